# Optimizing a Trainium2 kernel written in Bass

```python
import jax, jax.numpy as jnp
from jax import lax
import numpy as np


D_MODEL = 1024
BATCH = 4
SEQ = 4096
DEPTH = 1
DEC_BATCH = 128
DEC_SEQ = 4
PAST_LEN = 16384
PAGE_SIZE = 128

N_META = 16
HEAD_DIM = 64
N_ATTN_HEADS = 8
N_KV_HEADS = 2
GQA_REP = N_ATTN_HEADS // N_KV_HEADS
ATTN_WIDTH = N_ATTN_HEADS * HEAD_DIM
KV_WIDTH = N_KV_HEADS * HEAD_DIM
WINDOW = 128
BLOCK = 128
ROPE_THETA = 500000.0
ROPE_DIM = HEAD_DIM // 4
N_SSM_HEADS = 8
SSM_HEAD_DIM = 64
SSM_WIDTH = N_SSM_HEADS * SSM_HEAD_DIM
SSM_GROUPS = 2
D_STATE = 128
CONV_W = 4
CONV_DIM = SSM_WIDTH + 2 * SSM_GROUPS * D_STATE
CHUNK = 128
MIX_WIDTH = ATTN_WIDTH + SSM_WIDTH
D_IN_PROJ = ATTN_WIDTH + 2 * KV_WIDTH + SSM_WIDTH + CONV_DIM + N_SSM_HEADS
D_FF = 4 * D_MODEL
EPS = 1e-6

kernel_name = 'hymba_swa_sink_ssd_decoder_step'


def rmsnorm(x, g):
    xf = x.astype(jnp.float32)
    y = xf * lax.rsqrt(jnp.mean(xf * xf, axis=-1, keepdims=True) + EPS)
    return (y * g.astype(jnp.float32)).astype(x.dtype)


def rope_partial(x, pos):
    half = ROPE_DIM // 2
    inv = jnp.float32(ROPE_THETA) ** (-jnp.arange(half, dtype=jnp.float32) * 2.0 / ROPE_DIM)
    ang = pos.astype(jnp.float32)[:, None] * inv[None, :]
    cos = jnp.cos(ang)[:, None, :]
    sin = jnp.sin(ang)[:, None, :]
    xr = x[..., :ROPE_DIM].astype(jnp.float32)
    x1, x2 = xr[..., :half], xr[..., half:]
    rot = jnp.concatenate([x1 * cos - x2 * sin, x2 * cos + x1 * sin], axis=-1)
    return jnp.concatenate([rot.astype(x.dtype), x[..., ROPE_DIM:]], axis=-1)


def project(h, g, w_in):
    u = rmsnorm(h, g) @ w_in
    cuts = [ATTN_WIDTH, ATTN_WIDTH + KV_WIDTH, ATTN_WIDTH + 2 * KV_WIDTH,
            ATTN_WIDTH + 2 * KV_WIDTH + SSM_WIDTH, ATTN_WIDTH + 2 * KV_WIDTH + SSM_WIDTH + CONV_DIM]
    q, k, v, z, xbc, dtr = jnp.split(u, cuts, axis=-1)
    b, t = h.shape[:2]
    return (q.reshape(b, t, N_ATTN_HEADS, HEAD_DIM), k.reshape(b, t, N_KV_HEADS, HEAD_DIM),
            v.reshape(b, t, N_KV_HEADS, HEAD_DIM), z, xbc, dtr)


def window_mask(qpos, kpos):
    d = qpos[..., :, None] - kpos[..., None, :]
    return (kpos[..., None, :] >= 0) & (d >= 0) & (d < WINDOW)


def attn_core(q, k, v, mask, sinks):
    s = jnp.einsum('bnqgrd,bnkgd->bngrqk', q, k).astype(jnp.float32) * (HEAD_DIM ** -0.5)
    s = jnp.where(mask[None, :, None, None], s, -jnp.inf)
    sink = sinks.astype(jnp.float32).reshape(N_KV_HEADS, GQA_REP)[None, None, :, :, None, None]
    m = jnp.maximum(jnp.max(s, axis=-1, keepdims=True), sink)
    p = jnp.exp(s - m)
    p = p / (jnp.sum(p, axis=-1, keepdims=True) + jnp.exp(sink - m))
    return jnp.einsum('bngrqk,bnkgd->bnqgrd', p.astype(v.dtype), v)


def swa_prompt(q, k, v, sinks):
    b, L = q.shape[:2]
    pad = (-L) % BLOCK
    nb = (L + pad) // BLOCK
    fp = lambda a: jnp.pad(a, ((0, 0), (pad, 0)) + ((0, 0),) * (a.ndim - 2))
    qb = fp(q).reshape(b, nb, BLOCK, N_KV_HEADS, GQA_REP, HEAD_DIM)
    kb = fp(k).reshape(b, nb, BLOCK, N_KV_HEADS, HEAD_DIM)
    vb = fp(v).reshape(b, nb, BLOCK, N_KV_HEADS, HEAD_DIM)
    band = lambda a: jnp.concatenate(
        [jnp.pad(a[:, :-1], ((0, 0), (1, 0), (0, 0), (0, 0), (0, 0))), a], axis=2)
    qpos = jnp.arange(nb * BLOCK).reshape(nb, BLOCK) - pad
    kpos = (jnp.arange(nb)[:, None] - 1) * BLOCK + jnp.arange(2 * BLOCK)[None, :] - pad
    o = attn_core(qb, band(kb), band(vb), window_mask(qpos, kpos), sinks)
    return o.reshape(b, nb * BLOCK, ATTN_WIDTH)[:, pad:]


def swa_sample(q, k, v, cache_k, cache_v, sinks):
    b, t = q.shape[:2]
    wc = cache_k.shape[1]
    kk = jnp.concatenate([cache_k.astype(k.dtype), k], axis=1)
    vv = jnp.concatenate([cache_v.astype(v.dtype), v], axis=1)
    qpos = PAST_LEN + jnp.arange(t)
    kpos = PAST_LEN - wc + jnp.arange(wc + t)
    o = attn_core(q.reshape(b, 1, t, N_KV_HEADS, GQA_REP, HEAD_DIM), kk[:, None], vv[:, None],
                  window_mask(qpos, kpos)[None], sinks)
    return o.reshape(b, t, ATTN_WIDTH), kk[:, -wc:], vv[:, -wc:]


def causal_conv(x, prev, conv_w, conv_b):
    t = x.shape[1]
    xp = jnp.concatenate([prev, x], axis=1)
    y = sum((xp[:, j:j + t] * conv_w[j] for j in range(CONV_W)), conv_b)
    return jax.nn.silu(y), xp[:, t:]


def ssd_scan(x, dt, a, bm, cm, h0, chunk):
    b, L, H, P = x.shape
    nc = L // chunk
    rep = H // SSM_GROUPS
    bh = jnp.repeat(bm, rep, axis=2).reshape(b, nc, chunk, H, D_STATE)
    ch = jnp.repeat(cm, rep, axis=2).reshape(b, nc, chunk, H, D_STATE)
    xdt = (x * dt[..., None]).reshape(b, nc, chunk, H, P)
    acum = jnp.cumsum((dt * a).reshape(b, nc, chunk, H), axis=2)
    causal = jnp.tril(jnp.ones((chunk, chunk), dtype=bool))[None, None, :, :, None]
    seg = acum[:, :, :, None, :] - acum[:, :, None, :, :]
    decay = jnp.exp(jnp.where(causal, seg, -jnp.inf))
    g = jnp.einsum('bcthn,bcshn->bctsh', ch, bh) * decay
    y_diag = jnp.einsum('bctsh,bcshp->bcthp', g, xdt)
    to_end = jnp.exp(acum[:, :, -1:, :] - acum)
    s_chunk = jnp.einsum('bclhn,bclh,bclhp->bchpn', bh, to_end, xdt)
    total = jnp.exp(acum[:, :, -1, :])

    def step(h, inp):
        s_c, tot_c = inp
        return h * tot_c[:, :, None, None] + s_c, h

    h_final, h_in = lax.scan(step, h0, (jnp.moveaxis(s_chunk, 1, 0), jnp.moveaxis(total, 1, 0)))
    h_in = jnp.moveaxis(h_in, 0, 1)
    y_off = jnp.einsum('bclhn,bchpn->bclhp', ch, h_in) * jnp.exp(acum)[..., None]
    return (y_diag + y_off).reshape(b, L, H, P), h_final


def ssm_mixer(z, xbc, dtr, conv_prev, h0, conv_w, conv_b, dt_bias, a_log, d_skip, g_out):
    b, t = z.shape[:2]
    xbc_c, conv_new = causal_conv(xbc, conv_prev.astype(xbc.dtype), conv_w, conv_b)
    xs, bm, cm = jnp.split(xbc_c.astype(jnp.float32), [SSM_WIDTH, SSM_WIDTH + SSM_GROUPS * D_STATE], axis=-1)
    xs = xs.reshape(b, t, N_SSM_HEADS, SSM_HEAD_DIM)
    bm = bm.reshape(b, t, SSM_GROUPS, D_STATE)
    cm = cm.reshape(b, t, SSM_GROUPS, D_STATE)
    dt = jax.nn.softplus(dtr.astype(jnp.float32) + dt_bias.astype(jnp.float32))
    a = -jnp.exp(a_log.astype(jnp.float32))
    chunk = min(CHUNK, t)
    pad = (-t) % chunk
    fp = lambda u: jnp.pad(u, ((0, 0), (pad, 0)) + ((0, 0),) * (u.ndim - 2))
    y, h_new = ssd_scan(fp(xs), fp(dt), a, fp(bm), fp(cm), h0.astype(jnp.float32), chunk)
    y = y[:, pad:] + xs * d_skip.astype(jnp.float32)[:, None]
    y = y.reshape(b, t, SSM_WIDTH) * jax.nn.silu(z.astype(jnp.float32))
    return rmsnorm(y, g_out).astype(z.dtype), conv_new, h_new


def mix_and_mlp(h, ao, so, attn_out_g, w_out, norm2_g, w_up, w_down):
    mix = jnp.concatenate([rmsnorm(ao, attn_out_g), so], axis=-1)
    h = h + mix @ w_out
    u = jax.nn.relu(rmsnorm(h, norm2_g) @ w_up)
    return h + (u * u) @ w_down


def setup_inputs(seed: int = 0) -> dict:
    key = jax.random.key(seed)
    ks = jax.random.split(key, 24)
    f32 = jnp.float32
    nrm = lambda k, shape, s=1.0: (jax.random.normal(k, shape, f32) * s).astype(f32)
    cache_w = min(WINDOW, PAST_LEN)
    dt0 = jnp.exp(jax.random.uniform(ks[12], (DEPTH, N_SSM_HEADS), f32) * (np.log(0.1) - np.log(0.001)) + np.log(0.001))
    return {
        'x_prompt': nrm(ks[0], (BATCH, SEQ, D_MODEL)),
        'x_sample': nrm(ks[1], (DEC_BATCH, DEC_SEQ, D_MODEL)),
        'cache_k': nrm(ks[2], (DEPTH, DEC_BATCH, cache_w, N_KV_HEADS, HEAD_DIM)),
        'cache_v': nrm(ks[3], (DEPTH, DEC_BATCH, cache_w, N_KV_HEADS, HEAD_DIM)),
        'state_conv': nrm(ks[4], (DEPTH, DEC_BATCH, CONV_W - 1, CONV_DIM)),
        'state_ssm': nrm(ks[5], (DEPTH, DEC_BATCH, N_SSM_HEADS, SSM_HEAD_DIM, D_STATE), 0.5),
        'meta_tokens': nrm(ks[6], (N_META, D_MODEL)),
        'norm1_g': 1.0 + nrm(ks[7], (DEPTH, D_MODEL), 0.02),
        'w_in': nrm(ks[8], (DEPTH, D_MODEL, D_IN_PROJ), D_MODEL ** -0.5),
        'attn_sinks': nrm(ks[9], (DEPTH, N_ATTN_HEADS), 0.5),
        'attn_out_g': 1.0 + nrm(ks[10], (DEPTH, ATTN_WIDTH), 0.02),
        'conv_w': nrm(ks[11], (DEPTH, CONV_W, CONV_DIM), CONV_W ** -0.5),
        'conv_b': nrm(ks[13], (DEPTH, CONV_DIM), 0.02),
        'dt_bias': dt0 + jnp.log(-jnp.expm1(-dt0)),
        'a_log': jnp.log(jax.random.uniform(ks[14], (DEPTH, N_SSM_HEADS), f32, 1.0, 16.0)),
        'd_skip': 1.0 + nrm(ks[15], (DEPTH, N_SSM_HEADS), 0.1),
        'ssm_out_g': 1.0 + nrm(ks[16], (DEPTH, SSM_WIDTH), 0.02),
        'w_out': nrm(ks[17], (DEPTH, MIX_WIDTH, D_MODEL), MIX_WIDTH ** -0.5),
        'norm2_g': 1.0 + nrm(ks[18], (DEPTH, D_MODEL), 0.02),
        'w_up': nrm(ks[19], (DEPTH, D_MODEL, D_FF), D_MODEL ** -0.5),
        'w_down': nrm(ks[20], (DEPTH, D_FF, D_MODEL), D_FF ** -0.5),
        'norm_f_g': 1.0 + nrm(ks[21], (D_MODEL,), 0.02),
    }


def reference(x_prompt, x_sample, cache_k, cache_v, state_conv, state_ssm, meta_tokens, norm1_g, w_in,
              attn_sinks, attn_out_g, conv_w, conv_b, dt_bias, a_log, d_skip, ssm_out_g, w_out, norm2_g,
              w_up, w_down, norm_f_g):
    b_p = x_prompt.shape[0]
    meta = jnp.broadcast_to(meta_tokens.astype(x_prompt.dtype)[None], (b_p, N_META, D_MODEL))
    hp = jnp.concatenate([meta, x_prompt], axis=1)
    hs = x_sample
    L = hp.shape[1]
    t_s = hs.shape[1]
    pos_p = jnp.arange(L)
    pos_s = PAST_LEN + jnp.arange(t_s)
    kp_l, vp_l, cp_l, sp_l, ks_l, vs_l, cs_l, ss_l = ([] for _ in range(8))
    for i in range(DEPTH):
        q, k, v, z, xbc, dtr = project(hp, norm1_g[i], w_in[i])
        q = rope_partial(q, pos_p)
        k = rope_partial(k, pos_p)
        ao = swa_prompt(q, k, v, attn_sinks[i])
        conv0 = jnp.zeros((b_p, CONV_W - 1, CONV_DIM), xbc.dtype)
        h0 = jnp.zeros((b_p, N_SSM_HEADS, SSM_HEAD_DIM, D_STATE), jnp.float32)
        so, conv_n, ssm_n = ssm_mixer(z, xbc, dtr, conv0, h0, conv_w[i], conv_b[i], dt_bias[i], a_log[i],
                                      d_skip[i], ssm_out_g[i])
        hp = mix_and_mlp(hp, ao, so, attn_out_g[i], w_out[i], norm2_g[i], w_up[i], w_down[i])
        kp_l.append(k[:, L - WINDOW:])
        vp_l.append(v[:, L - WINDOW:])
        cp_l.append(conv_n)
        sp_l.append(ssm_n)
        q, k, v, z, xbc, dtr = project(hs, norm1_g[i], w_in[i])
        q = rope_partial(q, pos_s)
        k = rope_partial(k, pos_s)
        ao, k_win, v_win = swa_sample(q, k, v, cache_k[i], cache_v[i], attn_sinks[i])
        so, conv_n, ssm_n = ssm_mixer(z, xbc, dtr, state_conv[i], state_ssm[i], conv_w[i], conv_b[i],
                                      dt_bias[i], a_log[i], d_skip[i], ssm_out_g[i])
        hs = mix_and_mlp(hs, ao, so, attn_out_g[i], w_out[i], norm2_g[i], w_up[i], w_down[i])
        ks_l.append(k_win)
        vs_l.append(v_win)
        cs_l.append(conv_n)
        ss_l.append(ssm_n)
    y_prompt = rmsnorm(hp, norm_f_g)[:, N_META:]
    y_sample = rmsnorm(hs, norm_f_g)
    return (y_prompt, y_sample, jnp.stack(kp_l), jnp.stack(vp_l), jnp.stack(cp_l), jnp.stack(sp_l),
            jnp.stack(ks_l), jnp.stack(vs_l), jnp.stack(cs_l), jnp.stack(ss_l))
```

```python
import contextlib
import numpy as np
import concourse.bass as bass
import concourse.mybir as mybir
from concourse.bass_utils import run_bass_kernel_spmd

F32 = mybir.dt.float32
BF16 = mybir.dt.bfloat16
AF = mybir.ActivationFunctionType
ALU = mybir.AluOpType

D = 1024
NIN = 2312
NB_FULL = 16
NB_WARM = 17
SG = 8
PAST = 16384
THETA = 500000.0
EPS = 1e-6
ENABLE_SAMPLE = True
DEBUG = False
DBG_OUT = []


class Buf:
    def __init__(self, name, ap):
        self.name = name; self.t = ap; self.w = None; self.r = {}
        self.dsem = None; self.dcnt = 0; self.rsem = None; self.rcnt = 0

    def __getitem__(self, k):
        return self.t[k]


class KB:
    def __init__(self, nc, es):
        self.nc = nc; self.es = es
        self.eng = {'pe': nc.tensor, 'act': nc.scalar, 'dve': nc.vector, 'pool': nc.gpsimd, 'sp': nc.sync}
        self.sem = {e: es.enter_context(nc.semaphore("sem_" + e)) for e in self.eng}
        self.cnt = {e: 0 for e in self.eng}
        self.known = {e: {} for e in self.eng}
        self.dma_toks = {}

    def newsem(self, name):
        return self.es.enter_context(self.nc.semaphore(name))

    def _wait(self, e, tok):
        sem, val = tok
        if e == 'pe' and sem is self.sem['pe']:
            return
        k = self.known[e]
        if k.get(id(sem), 0) >= val:
            return
        k[id(sem)] = val
        self.eng[e].wait_ge(sem, val)

    def _deps(self, e, reads, writes):
        for b in reads:
            if b.w: self._wait(e, b.w)
        for b in writes:
            if b.w: self._wait(e, b.w)
            for t in b.r.values(): self._wait(e, t)

    def op(self, e, reads, writes, fn, fin=True):
        self._deps(e, reads, writes)
        ins = fn(self.eng[e])
        if fin or e != 'pe':
            self.cnt[e] += 1
            ins.then_inc(self.sem[e], 1)
            tok = (self.sem[e], self.cnt[e])
            self.pe_pending = False if e == 'pe' else getattr(self, 'pe_pending', False)
        else:
            tok = (self.sem[e], self.cnt[e] + 1)
            self.pe_pending = True
        for b in writes:
            b.w = tok; b.r = {}
        for b in reads:
            if b not in writes: b.r[e] = tok
        return ins

    def load(self, q, buf, out_ap, in_ap, **kw):
        self._deps(q, [], [buf])
        if buf.dsem is None: buf.dsem = self.newsem("d_" + buf.name)
        ins = self.eng[q].dma_start(out=out_ap, in_=in_ap, **kw)
        buf.dcnt += 16
        ins.then_inc(buf.dsem, 16)
        buf.w = (buf.dsem, buf.dcnt); buf.r = {}
        self.dma_toks[id(buf.dsem)] = buf.w

    def store(self, q, buf, out_ap, in_ap, **kw):
        self._deps(q, [buf], [])
        if buf.rsem is None: buf.rsem = self.newsem("r_" + buf.name)
        ins = self.eng[q].dma_start(out=out_ap, in_=in_ap, **kw)
        buf.rcnt += 16
        ins.then_inc(buf.rsem, 16)
        buf.r['st'] = (buf.rsem, buf.rcnt)
        self.dma_toks[id(buf.rsem)] = buf.r['st']

    def barrier(self):
        assert not getattr(self, 'pe_pending', False)
        for e in self.eng:
            for f in self.eng:
                if f != e and self.cnt[f] > 0:
                    self._wait(e, (self.sem[f], self.cnt[f]))
            for t in self.dma_toks.values():
                self._wait(e, t)

    def finish(self, q):
        for t in self.dma_toks.values():
            self._wait(q, t)


def bc(ap, shape):
    return ap.to_broadcast(list(shape))


def build_program():
    nc = bass.Bass("TRN2", target_bir_lowering=False)

    def din(name, shape):
        return nc.dram_tensor(name, list(shape), F32, kind="ExternalInput").ap()

    def dout(name, shape):
        return nc.dram_tensor(name, list(shape), F32, kind="ExternalOutput").ap()

    xw = din("xw", [NB_WARM * 128, D])
    xf = din("xf", [NB_FULL * 128, D])
    xsm = din("xsm", [64, D])
    rowmask_d = din("rowmask", [128, NB_WARM])
    rope_d = din("rope", [128, 18, 32])
    smask_d = din("smask", [128, 256])
    w_in = din("w_in", [D, NIN]); w_out = din("w_out", [D, D])
    w_up = din("w_up", [D, 4096]); w_down = din("w_down", [4096, D])
    norm1_g = din("norm1_g", [D]); norm2_g = din("norm2_g", [D]); norm_f_g = din("norm_f_g", [D])
    attn_out_g = din("attn_out_g", [512]); ssm_out_g = din("ssm_out_g", [512])
    attn_sinks = din("attn_sinks", [8]); dt_bias = din("dt_bias", [8]); a_log = din("a_log", [8]); d_skip = din("d_skip", [8])
    conv_w = din("conv_w", [4, D]); conv_b = din("conv_b", [D])
    cache_k = din("cache_k", [16, 128, 128]); cache_v = din("cache_v", [16, 128, 128])
    state_conv = din("state_conv", [16, 3, D]); state_ssm = din("state_ssm", [16, 512, 128])

    y_f = dout("y_f", [NB_FULL * 128, D]); y_s = dout("y_s", [64, D])
    kp_o = dout("kp_o", [128, 128]); vp_o = dout("vp_o", [128, 128])
    cp_o = dout("cp_o", [3, D]); sp_o = dout("sp_o", [512, 128])
    ks_o = dout("ks_o", [16, 128, 128]); vs_o = dout("vs_o", [16, 128, 128])
    cs_o = dout("cs_o", [16, 3, D]); ss_o = dout("ss_o", [16, 512, 128])

    es = contextlib.ExitStack()
    with es:
        kb = KB(nc, es)
        AW = 53200
        arena = es.enter_context(nc.sbuf_tensor("arena", [128, AW], F32))
        cur = [0]

        def alloc(name, shape, dt, at=None):
            n = int(np.prod(shape[1:]))
            words = n if dt == F32 else (n + 1) // 2
            words = (words + 7) // 8 * 8
            if at is None:
                off = cur[0]; cur[0] += words
            else:
                off = at
            assert off + words <= AW, (name, off, words)
            ap = arena[:, off:off + words]
            if dt == BF16:
                ap = ap.bitcast(BF16)
            ap = ap[:, 0:n]
            if len(shape) == 3:
                ap = ap.rearrange("p (a b) -> p a b", a=shape[1])
            elif len(shape) == 4:
                ap = ap.rearrange("p (a b c) -> p a b c", a=shape[1], b=shape[2])
            b = Buf(name, ap); b.off = off; b.words = words
            return b

        banks = []
        for i in range(8):
            t = es.enter_context(nc.psum_tensor("bank%d" % i, [128, 512], F32))
            b = Buf("bank%d" % i, t)
            b.f = t[:, :]
            b.b = t[:, :].bitcast(BF16)
            banks.append(b)
        T0, B1, B2, B3, B4, B5, B6, B7 = banks

        ident = alloc("ident", [128, 128], BF16)
        identf = alloc("identf", [128, 128], F32)
        trif = alloc("trif", [128, 128], F32)
        onesf = alloc("onesf", [128, 128], F32)
        maskd = alloc("maskd", [128, 128], BF16)
        maskp = alloc("maskp", [128, 128], BF16)
        rope = alloc("rope", [128, 18, 32], F32)
        rowmask = alloc("rowmask", [128, NB_WARM], F32)
        cbrow = alloc("cbrow", [1, 1024], BF16)
        onesrow = alloc("onesrow", [1, 512], BF16)
        prm = alloc("prm", [128, 64], F32)
        cw = alloc("cw", [128, 4, 8], F32)
        cb = alloc("cb", [128, 8], F32)
        gsc = alloc("gsc", [128, 24], F32)
        Hs = alloc("Hs", [128, D], F32)
        HnTs = alloc("HnTs", [128, 8, 64], BF16)
        cbuf = alloc("cbuf", [128, 8, 131], BF16)
        kT = [alloc("kT%d" % i, [128, 2, 128], BF16) for i in range(2)]
        Va = [alloc("Va%d" % i, [128, 2, 65], BF16) for i in range(2)]
        hT = alloc("hT", [128, 8, 64], F32)
        hTb = alloc("hTb", [128, 8, 64], BF16)
        Win = alloc("Win", [128, 8, NIN], BF16)
        Wout = alloc("Wout", [128, 8, D], BF16)
        H = alloc("H", [128, SG, D], F32)
        HnT = alloc("HnT", [128, 8, SG * 128], BF16)
        Hb = [Buf('H%d' % i, H.t[:, i, :]) for i in range(SG)]
        regX = cur[0]
        Wq0u = alloc("Wq0u", [128, 8, 1024], BF16)
        Wq0d = alloc("Wq0d", [128, 8, 1024], BF16)
        regA = cur[0]

        xs2 = [alloc("xs%d" % i, [128, D], F32) for i in range(2)]
        G1 = alloc("G1", [128, D], BF16)
        G2 = alloc("G2", [128, 8, 128], BF16)
        G2b = alloc("G2b", [128, 8, 128], BF16)
        G2s = [G2, G2b]
        gsel = {"g2": G2}
        G4 = alloc("G4", [128, 8, 128], BF16)
        G5 = alloc("G5", [128, 8, 128], BF16)
        G6 = alloc("G6", [128, D], BF16)
        G7 = alloc("G7", [128, 8, 128], BF16)
        G7b = alloc("G7b", [128, 8, 128], BF16)
        F1 = alloc("F1", [128, 8, 128], F32)
        Dg = alloc("Dg", [128, 8, 4, 128], BF16)
        F2 = alloc("F2", [128, 8, 128], F32)
        st = alloc("st", [128, 32], F32)
        qtm = alloc("qtm", [128, 8, 64], BF16)
        ta = alloc("ta", [128, 8, 16], F32)
        tb = alloc("tb", [128, 8, 16], F32)
        qT = alloc("qT", [128, 8, 128], BF16)
        kf = alloc("kf", [128, 2, 64], F32)
        vf = alloc("vf", [128, 2, 64], F32)
        ktm = alloc("ktm", [128, 2, 64], BF16)
        PT = alloc("PT", [128, 2, 8, 128], BF16)
        PTb = [Buf("PTs%d" % i, PT.t) for i in range(4)]
        xsB = alloc("xsB", [128, 768], BF16)
        xdt = alloc("xdt", [128, 8, 64], BF16)
        xdte = alloc("xdte", [128, 8, 64], BF16)
        xskip = alloc("xskip", [128, 8, 64], BF16)
        dts = alloc("dts", [128, 64], F32)
        GTm = alloc("GTm", [128, 2, 128], BF16)
        tmpv = F2.t[:, 4:8, :].rearrange("p a (b c) -> p (a b) c", c=64)
        xtv = F1.t.rearrange("p a b -> p (a b)")
        endA = cur[0]
        cur[0] = regA
        Wq1u = alloc("Wq1u", [128, 8, 1024], BF16)
        Wq1d = alloc("Wq1d", [128, 8, 1024], BF16)
        uT = [alloc("uT%d" % i, [128, 8, 512], BF16) for i in range(2)]
        ur = [alloc("ur%d" % i, [128, 512], BF16) for i in range(2)]
        gfb = alloc("gfb", [128, D], F32)
        stB = alloc("stB", [128, 8], F32)
        endB = cur[0]
        assert max(endA, endB) <= AW, (endA, endB, AW)
        Wq = [(Wq0u, Wq0d), (Wq1u, Wq1d)]

        def dbg(name, buf, ap, q='sp'):
            if not DEBUG:
                return
            dt = ap.dtype
            t = nc.dram_tensor("dbg_" + name, list(ap.shape), dt, kind="ExternalOutput").ap()
            DBG_OUT.append("dbg_" + name)
            kb.store(q, buf, t, ap)

        def pool_mask(buf, pattern, cmp, cm, base=0, fill=0.0):
            kb.op('pool', [buf], [buf], lambda e: e.affine_select(out=buf[:], in_=buf[:], pattern=pattern, compare_op=cmp, fill=fill, base=base, channel_multiplier=cm))

        kb.op('pool', [], [identf], lambda e: e.memset(identf[:], 1.0))
        pool_mask(identf, [[-1, 128]], ALU.is_equal, 1)
        kb.op('pool', [], [trif], lambda e: e.memset(trif[:], 1.0))
        pool_mask(trif, [[1, 128]], ALU.is_ge, -1)
        kb.op('pool', [], [onesf], lambda e: e.memset(onesf[:], 1.0))
        kb.op('dve', [identf], [ident], lambda e: e.tensor_copy(out=ident[:], in_=identf[:]))
        kb.op('dve', [trif], [maskd], lambda e: e.tensor_copy(out=maskd[:], in_=trif[:]))
        kb.op('dve', [trif], [maskp], lambda e: e.tensor_scalar(out=maskp[:], in0=trif[:], scalar1=-1.0, scalar2=1.0, op0=ALU.mult, op1=ALU.add))
        kb.load('sp', rope, rope[:], rope_d)
        kb.load('sp', rowmask, rowmask[:], rowmask_d)
        kb.load('pool', cbrow, cbrow[0:1, :], bass.AP(conv_b.tensor, conv_b.offset, [[0, 1], [1, 1024]]))
        kb.op('dve', [], [onesrow], lambda e: e.memset(onesrow[:], 1.0))

        def pbc(v, n):
            return bass.AP(v.tensor, v.offset, [[0, 128], [1, n]])
        kb.load('sp', prm, prm[:, 0:8], pbc(dt_bias, 8))
        kb.load('sp', prm, prm[:, 8:16], pbc(a_log, 8))
        kb.load('sp', prm, prm[:, 16:24], pbc(d_skip, 8))
        kb.load('sp', prm, prm[:, 24:32], pbc(attn_sinks, 8))
        for jj in range(4):
            kb.load('sp', cw, cw[:, jj, :], conv_w[jj].rearrange("(ct p) -> p ct", p=128), allow_slow_non_contiguous=True)
        kb.load('sp', cb, cb[:], conv_b.rearrange("(ct p) -> p ct", p=128), allow_slow_non_contiguous=True)
        kb.load('sp', gsc, gsc[:, 0:8], norm1_g.rearrange("(kt p) -> p kt", p=128), allow_slow_non_contiguous=True)
        kb.load('sp', gsc, gsc[:, 8:12], attn_out_g.rearrange("(kt p) -> p kt", p=128), allow_slow_non_contiguous=True)
        kb.load('sp', gsc, gsc[:, 12:16], ssm_out_g.rearrange("(kt p) -> p kt", p=128), allow_slow_non_contiguous=True)
        kb.load('sp', gsc, gsc[:, 16:24], norm2_g.rearrange("(kt p) -> p kt", p=128), allow_slow_non_contiguous=True)
        kb.op('dve', [prm], [prm], lambda e: e.memset(prm[:, 48:49], EPS))
        kb.op('dve', [prm], [prm], lambda e: e.memset(prm[:, 49:50], 1.0))
        kb.op('act', [prm], [prm], lambda e: e.activation(out=prm[:, 32:40], in_=prm[:, 8:16], func=AF.Exp))
        kb.op('dve', [prm], [prm], lambda e: e.tensor_scalar(out=prm[:, 32:40], in0=prm[:, 32:40], scalar1=-1.0, scalar2=None, op0=ALU.mult))
        kb.op('act', [prm], [prm], lambda e: e.activation(out=prm[:, 40:48], in_=prm[:, 24:32], func=AF.Exp))
        epsb = prm[:, 48:49]

        w_in_v = w_in.rearrange("(kt p) n -> p kt n", p=128)
        for (c0, c1) in [(1280, 2312), (0, 640), (640, 1280)]:
            kb.load('pool', Win, Win[:, :, c0:c1], w_in_v[:, :, c0:c1])
        kb.load('pool', Wout, Wout[:], w_out.rearrange("(kt p) n -> p kt n", p=128))
        for kt in range(8):
            kb.op('dve', [Win, gsc], [Win], lambda e: e.tensor_scalar(out=Win[:, kt, :], in0=Win[:, kt, :], scalar1=gsc[:, kt:kt + 1], scalar2=None, op0=ALU.mult))
        for kt in range(8):
            kb.op('dve', [Wout, gsc], [Wout], lambda e: e.tensor_scalar(out=Wout[:, kt, :], in0=Wout[:, kt, :], scalar1=gsc[:, 8 + kt:9 + kt], scalar2=None, op0=ALU.mult))

        kb.barrier()
        w_up_v = w_up.rearrange("(kt p) f -> p kt f", p=128)
        w_dn_v = w_down.rearrange("(ft p) d -> p ft d", p=128)

        def load_quarter(bufs, c):
            bu, bd = bufs
            kb.load('pool', bu, bu[:], w_up_v[:, :, c * 1024:(c + 1) * 1024])
            kb.load('pool', bd, bd[:], w_dn_v[:, c * 8:(c + 1) * 8, :])
            for kt in range(8):
                kb.op('dve', [bu, gsc], [bu], lambda e: e.tensor_scalar(out=bu[:, kt, :], in0=bu[:, kt, :], scalar1=gsc[:, 16 + kt:17 + kt], scalar2=None, op0=ALU.mult))

        def rstd_from_sumsq(col_in, col_out, M, n):
            kb.op('act', [st, prm], [st], lambda e: e.activation(out=st[0:M, col_out:col_out + 1], in_=st[0:M, col_in:col_in + 1], func=AF.Ln, scale=1.0 / n, bias=epsb[0:M, :]))
            kb.op('act', [st], [st], lambda e: e.activation(out=st[0:M, col_out:col_out + 1], in_=st[0:M, col_out:col_out + 1], func=AF.Exp, scale=-0.5))

        def transpose_to(dst_buf, dst_ap3, src_buf, src2d, M, nkt):
            for kt in range(nkt):
                kb.op('pe', [src_buf, ident], [T0], lambda e: e.transpose(out=T0.b[:, kt * 128:kt * 128 + M], in_=src2d[:, kt * 128:(kt + 1) * 128], identity=ident[0:M, 0:M]))
            kb.op('act', [T0], [dst_buf], lambda e: e.copy(out=dst_ap3, in_=T0.b[:, 0:nkt * 128].rearrange("p (a b) -> p a b", a=nkt)[:, :, 0:M]))

        def norm_and_T(xsb, M):
            kb.op('act', [xsb], [G1, st], lambda e: e.activation(out=G1[0:M, :], in_=xsb[0:M, :], func=AF.Square, accum_out=st[0:M, 0:1]))
            rstd_from_sumsq(0, 1, M, D)
            kb.op('dve', [xsb, st], [G1], lambda e: e.tensor_scalar(out=G1[0:M, :], in0=xsb[0:M, :], scalar1=st[0:M, 1:2], scalar2=None, op0=ALU.mult))
            transpose_to(G7, G7[:, :, 0:M], G1, G1[0:M, :], M, 8)

        def proj_tok(bank, ncols, c0, M, co=0):
            for kt in range(8):
                kb.op('pe', [G7, Win], [bank], lambda e: e.matmul(bank.f[0:M, co:co + ncols], lhsT=G7[:, kt, 0:M], rhs=Win[:, kt, c0:c0 + ncols], start=(kt == 0), stop=(kt == 7)), fin=(kt == 7))

        def proj_xbc(M, cts, bks=None):
            bks = (B4, B5) if bks is None else bks
            for ct in cts:
                bank = bks[0] if ct < 4 else bks[1]
                o = (ct % 4) * 128
                for kt in range(8):
                    kb.op('pe', [G7, Win], [bank], lambda e: e.matmul(bank.f[:, o:o + M], lhsT=Win[:, kt, 1280 + ct * 128:1280 + (ct + 1) * 128], rhs=G7[:, kt, 0:M], start=(kt == 0), stop=(kt == 7)), fin=(kt == 7))

        def rope_apply(bank, c0, nh, slot, M, dst32, dst32buf, dst16, dst16buf):
            src = bank.f[0:M, c0:c0 + nh * 64].rearrange("p (h d) -> p h d", h=nh)
            cc = bc(rope[0:M, slot:slot + 1, 0:16], [M, nh, 16])
            ns = bc(rope[0:M, slot:slot + 1, 16:24], [M, nh, 8])
            ps = bc(rope[0:M, slot:slot + 1, 24:32], [M, nh, 8])
            kb.op('dve', [bank, rope], [ta], lambda e: e.tensor_tensor(out=ta[0:M, 0:nh, :], in0=src[:, :, 0:16], in1=cc, op=ALU.mult))
            kb.op('dve', [bank, rope], [tb], lambda e: e.tensor_tensor(out=tb[0:M, 0:nh, 0:8], in0=src[:, :, 8:16], in1=ns, op=ALU.mult))
            kb.op('dve', [bank, rope], [tb], lambda e: e.tensor_tensor(out=tb[0:M, 0:nh, 8:16], in0=src[:, :, 0:8], in1=ps, op=ALU.mult))
            if dst32 is not None:
                kb.op('dve', [ta, tb], [dst32buf], lambda e: e.tensor_tensor(out=dst32[:, :, 0:16], in0=ta[0:M, 0:nh, :], in1=tb[0:M, 0:nh, :], op=ALU.add))
                kb.op('act', [bank], [dst32buf], lambda e: e.copy(out=dst32[:, :, 16:64], in_=src[:, :, 16:64]))
                kb.op('act', [dst32buf], [dst16buf], lambda e: e.copy(out=dst16, in_=dst32))
            else:
                kb.op('dve', [ta, tb], [dst16buf], lambda e: e.tensor_tensor(out=dst16[:, :, 0:16], in0=ta[0:M, 0:nh, :], in1=tb[0:M, 0:nh, :], op=ALU.add))
                kb.op('act', [bank], [dst16buf], lambda e: e.copy(out=dst16[:, :, 16:64], in_=src[:, :, 16:64]))

        def kv_block(bank, slot, M, par, onescol):
            rope_apply(bank, 0, 2, slot, M, kf[0:M], kf, ktm[0:M], ktm)
            kb.op('act', [bank], [vf], lambda e: e.copy(out=vf[0:M], in_=bank.f[0:M, 128:256].rearrange("p (h d) -> p h d", h=2)))
            kb.op('dve', [vf], [Va[par]], lambda e: e.tensor_copy(out=Va[par][0:M, :, 0:64], in_=vf[0:M]))
            kb.op('dve', [rowmask, prm], [Va[par]], lambda e: e.tensor_copy(out=Va[par][0:M, :, 64:65], in_=bc(onescol[0:M], [M, 2, 1])))
            for h in range(2):
                kb.op('pe', [ktm, ident], [T0], lambda e: e.transpose(out=T0.b[0:64, h * 128:h * 128 + M], in_=ktm[0:M, h, :], identity=ident[0:M, 0:M]))
            kb.op('act', [T0], [kT[par]], lambda e: e.copy(out=kT[par][0:64, :, 0:M], in_=T0.b[0:64, 0:256].rearrange("p (a b) -> p a b", a=2)[:, :, 0:M]))

        def build_dg():
            for ct in range(8):
                for j in range(4):
                    kb.op('dve', [ident, cw], [Dg], lambda e: e.tensor_scalar(out=Dg[:, ct, j, :], in0=ident[:, :], scalar1=cw[:, j, ct:ct + 1], scalar2=None, op0=ALU.mult))

        def conv_silu(bankA, bankB, M, cts, shift, cbw, g2=None):
            G2_ = gsel['g2'] if g2 is None else g2
            off = 3 * shift
            lo, hi = min(cts), max(cts) + 1
            if lo < 4:
                kb.op('act', [bankA], [cbw], lambda e: e.copy(out=cbw[:, lo:min(hi, 4), off:off + M], in_=bankA.f[:, :].rearrange("p (a b) -> p a b", a=4)[:, lo:min(hi, 4), 0:M]))
            if hi > 4:
                kb.op('dve', [bankB], [cbw], lambda e: e.tensor_copy(out=cbw[:, 4:hi, off:off + M], in_=bankB.f[:, :].rearrange("p (a b) -> p a b", a=4)[:, 0:hi - 4, 0:M]))
            for ct in range(lo, hi):
                bank = bankA if ct < 4 else bankB
                o = (ct % 4) * 128
                for j in range(4):
                    kb.op('pe', [Dg, cbw], [bank], lambda e: e.matmul(bank.f[:, o:o + M], lhsT=Dg[:, ct, j, :], rhs=cbw[:, ct, j * shift:j * shift + M], start=(j == 0), stop=False), fin=False)
                kb.op('pe', [cbrow, onesrow], [bank], lambda e: e.matmul(bank.f[:, o:o + M], lhsT=cbrow[0:1, ct * 128:(ct + 1) * 128], rhs=onesrow[0:1, 0:M], start=False, stop=True), fin=(ct == hi - 1 or ct == 3))
            halves = []
            if lo < 4: halves.append((bankA, lo, min(hi, 4), 0))
            if hi > 4: halves.append((bankB, 4, hi, 4))
            for bank, a, b_, base in halves:
                kb.op('act', [bank], [G2_], lambda e: e.activation(out=G2_[:, a:b_, 0:M], in_=bank.f[:, :].rearrange("p (a b) -> p a b", a=4)[:, a - base:b_ - base, 0:M], func=AF.Silu))

        def conv_shift(M):
            kb.op('pool', [cbuf], [cbuf], lambda e: e.tensor_copy(out=cbuf[:, :, 0:3], in_=cbuf[:, :, M:M + 3]))

        def xs_B_tokmajor(M):
            for ct in range(6):
                kb.op('pe', [gsel['g2'], ident], [T0], lambda e: e.transpose(out=T0.b[0:M, ct * 128:(ct + 1) * 128], in_=gsel['g2'][:, ct, 0:M], identity=ident[:, :]), fin=(ct == 5))
            kb.op('act', [T0], [xsB], lambda e: e.copy(out=xsB[0:M, :], in_=T0.b[0:M, 0:768]))

        def dt_compute(bank, c0, M, maskcol):
            kb.op('dve', [bank, prm], [dts], lambda e: e.tensor_tensor(out=dts[0:M, 0:8], in0=bank.f[0:M, c0:c0 + 8], in1=prm[0:M, 0:8], op=ALU.add))
            kb.op('act', [dts], [dts], lambda e: e.activation(out=dts[0:M, 0:8], in_=dts[0:M, 0:8], func=AF.Exp))
            kb.op('act', [dts, prm], [dts], lambda e: e.activation(out=dts[0:M, 0:8], in_=dts[0:M, 0:8], func=AF.Ln, bias=prm[0:M, 49:50]))
            if maskcol is not None:
                kb.op('dve', [dts, rowmask], [dts], lambda e: e.tensor_scalar(out=dts[0:M, 0:8], in0=dts[0:M, 0:8], scalar1=maskcol, scalar2=None, op0=ALU.mult))
            kb.op('dve', [dts, prm], [dts], lambda e: e.tensor_tensor(out=dts[0:M, 8:16], in0=dts[0:M, 0:8], in1=prm[0:M, 32:40], op=ALU.mult))

        def ssd_state_update(M, cummat, totmat, hTl, hTbl, bank=None, split=0, sc=256):
            B2_ = B2 if bank is None else bank
            if split in (0, 1):
                kb.op('pe', [dts], [B2_], lambda e: e.matmul(B2_.f[0:M, sc:sc + 8], lhsT=cummat, rhs=dts[0:M, 8:16], start=True, stop=True))
                kb.op('pe', [dts], [B2_], lambda e: e.matmul(B2_.f[0:128, sc + 8:sc + 16], lhsT=totmat, rhs=dts[0:M, 8:16], start=True, stop=True))
                kb.op('act', [B2_], [dts], lambda e: e.copy(out=dts[0:M, 24:32], in_=B2_.f[0:M, sc:sc + 8]))
                kb.op('act', [B2_], [dts], lambda e: e.copy(out=dts[:, 32:40], in_=B2_.f[:, sc + 8:sc + 16]))
                kb.op('act', [B2_], [dts], lambda e: e.activation(out=dts[:, 56:64], in_=B2_.f[:, sc + 8:sc + 16], func=AF.Exp))
                kb.op('dve', [dts], [dts], lambda e: e.tensor_tensor(out=dts[0:M, 40:48], in0=dts[0:M, 32:40], in1=dts[0:M, 24:32], op=ALU.subtract))
                kb.op('act', [dts], [dts], lambda e: e.activation(out=dts[0:M, 40:48], in_=dts[0:M, 40:48], func=AF.Exp))
                kb.op('dve', [dts], [dts], lambda e: e.tensor_tensor(out=dts[0:M, 48:56], in0=dts[0:M, 40:48], in1=dts[0:M, 0:8], op=ALU.mult))
            if split == 1:
                return
            if split == 2:
                xs3 = xsB[0:M, 0:512].rearrange("p (h d) -> p h d", h=8)
                kb.op('dve', [xsB, dts], [xdte], lambda e: e.tensor_tensor(out=xdte[0:M], in0=xs3, in1=bc(dts[0:M, 48:56].unsqueeze(2), [M, 8, 64]), op=ALU.mult))
                return
            kb.op('dve', [dts], [dts], lambda e: e.tensor_tensor(out=dts[0:M, 40:48], in0=dts[0:M, 32:40], in1=dts[0:M, 24:32], op=ALU.subtract))
            kb.op('act', [dts], [dts], lambda e: e.activation(out=dts[0:M, 40:48], in_=dts[0:M, 40:48], func=AF.Exp))
            kb.op('dve', [dts], [dts], lambda e: e.tensor_tensor(out=dts[0:M, 48:56], in0=dts[0:M, 40:48], in1=dts[0:M, 0:8], op=ALU.mult))
            xs3 = xsB[0:M, 0:512].rearrange("p (h d) -> p h d", h=8)
            kb.op('dve', [xsB, dts], [xdte], lambda e: e.tensor_tensor(out=xdte[0:M], in0=xs3, in1=bc(dts[0:M, 48:56].unsqueeze(2), [M, 8, 64]), op=ALU.mult))

        def state_matmul_and_update(M, bank=None):
            B3 = banks[3] if bank is None else bank
            for g in range(2):
                kb.op('pe', [xsB, xdte], [B3], lambda e: e.matmul(B3.f[:, g * 256:(g + 1) * 256], lhsT=xsB[0:M, 512 + g * 128:512 + (g + 1) * 128], rhs=xdte[0:M, 4 * g:4 * g + 4, :], start=True, stop=True))
            kb.op('dve', [hT, dts], [F2], lambda e: e.tensor_tensor(out=tmpv, in0=hT[:], in1=bc(dts[:, 56:64].unsqueeze(2), [128, 8, 64]), op=ALU.mult))
            kb.op('dve', [F2, B3], [hT], lambda e: e.tensor_tensor(out=hT[:], in0=tmpv, in1=B3.f[:, :].rearrange("p (h d) -> p h d", h=8), op=ALU.add))

        kb.op('dve', [], [hT], lambda e: e.memset(hT[:], 0.0))
        kb.op('dve', [], [hTb], lambda e: e.memset(hTb[:], 0.0))
        kb.op('dve', [], [cbuf], lambda e: e.memset(cbuf[:], 0.0))

        def warm_slot(s):
            xsb = xs2[s % 2]
            kb.load('sp', xsb, xsb[:], xw[s * 128:(s + 1) * 128, :])
            norm_and_T(xsb, 128)
            last = (s == NB_WARM - 1)
            proj_xbc(128, range(8) if last else range(6))
            if last:
                proj_tok(B2, 256, 512, 128)
            proj_tok(B1, 8, 2304, 128)
            if last:
                kv_block(B2, 16, 128, 1, rowmask[:, s:s + 1].unsqueeze(2))
            conv_silu(B4, B5, 128, range(8) if last else range(6), 1, cbuf)
            conv_shift(128)
            xs_B_tokmajor(128)
            dt_compute(B1, 0, 128, rowmask[:, s:s + 1])
            ssd_state_update(128, trif[:, :], onesf[:, :], hT, hTb)
            state_matmul_and_update(128)

        build_dg()

        wc = [H.off]

        def walloc(name, shape, dt):
            b = alloc(name, shape, dt, at=wc[0]); wc[0] += b.words
            assert wc[0] <= regA, (name, wc[0], regA)
            return b
        wx = walloc("wx", [128, 4, D], F32)
        wxb = walloc("wxb", [128, 4, D], BF16)
        wxT = walloc("wxT", [128, 8, 512], BF16)
        wcb = walloc("wcb", [128, 6, 515], BF16)
        wsg = walloc("wsg", [128, 3, 512], F32)
        wG2 = walloc("wG2", [128, 6, 512], BF16)
        wxsB = walloc("wxsB", [128, 4, 768], BF16)
        wdts = walloc("wdts", [128, 8, 32], F32)
        wxdte = walloc("wxdte", [128, 4, 512], BF16)
        wst = walloc("wst", [128, 16], F32)
        kb.op('dve', [], [wcb], lambda e: e.memset(wcb[:], 0.0))

        def warm_group(g):
            s0 = 4 * g
            kb.load('sp', wx, wx[:], xw[s0 * 128:(s0 + 4) * 128, :].rearrange("(s p) d -> p s d", p=128))
            for s in range(4):
                kb.op('act', [wx], [wxb, wst], lambda e: e.activation(out=wxb[:, s, :], in_=wx[:, s, :], func=AF.Square, accum_out=wst[:, s:s + 1]))
            kb.op('act', [wst, prm], [wst], lambda e: e.activation(out=wst[:, 4:8], in_=wst[:, 0:4], func=AF.Ln, scale=1.0 / D, bias=epsb))
            kb.op('act', [wst], [wst], lambda e: e.activation(out=wst[:, 4:8], in_=wst[:, 4:8], func=AF.Exp, scale=-0.5))
            for s in range(4):
                kb.op('dve', [wx, wst], [wxb], lambda e: e.tensor_scalar(out=wxb[:, s, :], in0=wx[:, s, :], scalar1=wst[:, 4 + s:5 + s], scalar2=None, op0=ALU.mult))
            for s in range(4):
                tbk = T0 if s % 2 == 0 else B7
                for kt in range(8):
                    kb.op('pe', [wxb, ident], [tbk], lambda e: e.transpose(out=tbk.b[:, kt * 128:(kt + 1) * 128], in_=wxb[:, s, kt * 128:(kt + 1) * 128], identity=ident[:, :]), fin=(kt == 7))
                if s % 2 == 0:
                    kb.op('act', [tbk], [wxT], lambda e: e.copy(out=wxT[:, :, s * 128:(s + 1) * 128], in_=tbk.b[:, :].rearrange("p (a b) -> p a b", a=8)))
                else:
                    kb.op('dve', [tbk], [wxT], lambda e: e.tensor_copy(out=wxT[:, :, s * 128:(s + 1) * 128], in_=tbk.b[:, :].rearrange("p (a b) -> p a b", a=8)))
            for s in range(4):
                for kt in range(8):
                    kb.op('pe', [wxT, Win], [B6], lambda e: e.matmul(B6.f[:, s * 8:(s + 1) * 8], lhsT=wxT[:, kt, s * 128:(s + 1) * 128], rhs=Win[:, kt, 2304:2312], start=(kt == 0), stop=(kt == 7)), fin=(kt == 7))
            bk4 = [B1, B2, B3, B4]
            for ct in range(6):
                bank = bk4[ct % 4]
                for kt in range(8):
                    kb.op('pe', [wxT, Win], [bank], lambda e: e.matmul(bank.f[:, :], lhsT=Win[:, kt, 1280 + ct * 128:1280 + (ct + 1) * 128], rhs=wxT[:, kt, :], start=(kt == 0), stop=(kt == 7)), fin=(kt == 7))
                if ct % 2 == 0:
                    kb.op('act', [bank], [wcb], lambda e: e.copy(out=wcb[:, ct, 3:515], in_=bank.f[:, :]))
                else:
                    kb.op('dve', [bank], [wcb], lambda e: e.tensor_copy(out=wcb[:, ct, 3:515], in_=bank.f[:, :]))
            v4 = lambda r: wdts[:, r, :].rearrange("p (s h) -> p s h", s=4)
            kb.op('dve', [B6, prm], [wdts], lambda e: e.tensor_tensor(out=v4(0), in0=B6.f[:, 0:32].rearrange("p (s h) -> p s h", s=4), in1=bc(prm[:, 0:8].unsqueeze(1), [128, 4, 8]), op=ALU.add))
            kb.op('act', [wdts], [wdts], lambda e: e.activation(out=wdts[:, 0, :], in_=wdts[:, 0, :], func=AF.Exp))
            kb.op('act', [wdts, prm], [wdts], lambda e: e.activation(out=wdts[:, 0, :], in_=wdts[:, 0, :], func=AF.Ln, bias=prm[:, 49:50]))
            kb.op('dve', [wdts, rowmask], [wdts], lambda e: e.tensor_tensor(out=v4(0), in0=v4(0), in1=bc(rowmask[:, s0:s0 + 4].unsqueeze(2), [128, 4, 8]), op=ALU.mult))
            kb.op('dve', [wdts, prm], [wdts], lambda e: e.tensor_tensor(out=v4(1), in0=v4(0), in1=bc(prm[:, 32:40].unsqueeze(1), [128, 4, 8]), op=ALU.mult))
            kb.op('pe', [wdts], [B5], lambda e: e.matmul(B5.f[:, 0:32], lhsT=trif[:, :], rhs=wdts[:, 1, :], start=True, stop=True))
            kb.op('pe', [wdts], [B5], lambda e: e.matmul(B5.f[:, 32:64], lhsT=onesf[:, :], rhs=wdts[:, 1, :], start=True, stop=True))
            kb.op('act', [B5], [wdts], lambda e: e.copy(out=wdts[:, 2:4, :].rearrange("p a b -> p (a b)"), in_=B5.f[:, 0:64]))
            kb.op('act', [B5], [wdts], lambda e: e.activation(out=wdts[:, 6, :], in_=B5.f[:, 32:64], func=AF.Exp))
            kb.op('dve', [wdts], [wdts], lambda e: e.tensor_tensor(out=wdts[:, 4, :], in0=wdts[:, 3, :], in1=wdts[:, 2, :], op=ALU.subtract))
            kb.op('act', [wdts], [wdts], lambda e: e.activation(out=wdts[:, 4, :], in_=wdts[:, 4, :], func=AF.Exp))
            kb.op('dve', [wdts], [wdts], lambda e: e.tensor_tensor(out=wdts[:, 5, :], in0=wdts[:, 4, :], in1=wdts[:, 0, :], op=ALU.mult))
            bk3 = [B1, B2, B3]
            for half in range(2):
                for i in range(3):
                    ct = 3 * half + i
                    bank = bk3[i]
                    for j in range(4):
                        kb.op('pe', [Dg, wcb], [bank], lambda e: e.matmul(bank.f[:, :], lhsT=Dg[:, ct, j, :], rhs=wcb[:, ct, j:j + 512], start=(j == 0), stop=False), fin=False)
                    kb.op('pe', [cbrow, onesrow], [bank], lambda e: e.matmul(bank.f[:, :], lhsT=cbrow[0:1, ct * 128:(ct + 1) * 128], rhs=onesrow[0:1, 0:512], start=False, stop=True))
                    kb.op('act', [bank], [wG2], lambda e: e.activation(out=wG2[:, ct, :], in_=bank.f[:, :], func=AF.Silu))
            kb.op('pool', [wcb], [wcb], lambda e: e.tensor_copy(out=wcb[:, :, 0:3], in_=wcb[:, :, 512:515]))
            for s in range(4):
                tbk = T0 if s % 2 == 0 else B7
                for ct in range(6):
                    kb.op('pe', [wG2, ident], [tbk], lambda e: e.transpose(out=tbk.b[:, ct * 128:(ct + 1) * 128], in_=wG2[:, ct, s * 128:(s + 1) * 128], identity=ident[:, :]), fin=(ct == 5))
                if s % 2 == 0:
                    kb.op('act', [tbk], [wxsB], lambda e: e.copy(out=wxsB[:, s, :], in_=tbk.b[:, 0:768]))
                else:
                    kb.op('dve', [tbk], [wxsB], lambda e: e.tensor_copy(out=wxsB[:, s, :], in_=tbk.b[:, 0:768]))
            kb.op('dve', [wxsB, wdts], [wxdte], lambda e: e.tensor_tensor(out=wxdte[:].rearrange("p s (h d) -> p s h d", h=8), in0=wxsB[:, :, 0:512].rearrange("p s (h d) -> p s h d", h=8), in1=bc(v4(5).unsqueeze(3), [128, 4, 8, 64]), op=ALU.mult))
            for s in range(4):
                bank = bk4[s]
                for gg in range(2):
                    kb.op('pe', [wxsB, wxdte], [bank], lambda e: e.matmul(bank.f[:, gg * 256:(gg + 1) * 256], lhsT=wxsB[:, s, 512 + gg * 128:512 + (gg + 1) * 128], rhs=wxdte[:, s, gg * 256:(gg + 1) * 256], start=True, stop=True))
                kb.op('dve', [hT, wdts], [F2], lambda e: e.tensor_tensor(out=tmpv, in0=hT[:], in1=bc(wdts[:, 6, s * 8:(s + 1) * 8].unsqueeze(2), [128, 8, 64]), op=ALU.mult))
                kb.op('dve', [F2, bank], [hT], lambda e: e.tensor_tensor(out=hT[:], in0=tmpv, in1=bank.f[:, :].rearrange("p (h d) -> p h d", h=8), op=ALU.add))

        for g in range(4):
            warm_group(g)
        kb.op('pool', [wcb], [cbuf], lambda e: e.tensor_copy(out=cbuf[:, 0:6, 0:3], in_=wcb[:, :, 0:3]))
        warm_slot(NB_WARM - 1)
        kb.op('act', [hT], [hTb], lambda e: e.copy(out=hTb[:], in_=hT[:]))

        def attn_finish(M):
            g6v = G6[0:M, 0:512].rearrange("p (h d) -> p h d", h=8)
            for kvh in range(2):
                bank = B1 if kvh == 0 else B2
                o3 = bank.f[0:M, 0:260].rearrange("p (h d) -> p h d", h=4)
                kb.op('dve', [bank, prm], [st], lambda e: e.tensor_tensor(out=st[0:M, 8 + 4 * kvh:12 + 4 * kvh].unsqueeze(2), in0=o3[:, :, 64:65], in1=prm[0:M, 40 + 4 * kvh:44 + 4 * kvh].unsqueeze(2), op=ALU.add))
                kb.op('dve', [st], [st], lambda e: e.reciprocal(out=st[0:M, 8 + 4 * kvh:12 + 4 * kvh], in_=st[0:M, 8 + 4 * kvh:12 + 4 * kvh]))
                kb.op('dve', [bank, st], [G6], lambda e: e.tensor_tensor(out=g6v[:, 4 * kvh:4 * kvh + 4, :], in0=o3[:, :, 0:64], in1=bc(st[0:M, 8 + 4 * kvh:12 + 4 * kvh].unsqueeze(2), [M, 4, 64]), op=ALU.mult))
            kb.op('act', [G6], [qtm, st], lambda e: e.activation(out=qtm[0:M].rearrange("p h d -> p (h d)"), in_=G6[0:M, 0:512], func=AF.Square, accum_out=st[0:M, 2:3]))
            rstd_from_sumsq(2, 3, M, 512)
            kb.op('dve', [G6, st], [G6], lambda e: e.tensor_scalar(out=G6[0:M, 0:512], in0=G6[0:M, 0:512], scalar1=st[0:M, 3:4], scalar2=None, op0=ALU.mult))

        def ssd_intra_g(M, cummat_bcast_rhs, gmask, split=False):
            for h in range(8):
                bank = B4 if h < 4 else B5
                o = (h % 4) * 128
                kb.op('pe', [dts], [bank], lambda e: e.matmul(bank.f[0:M, o:o + M], lhsT=bc(dts[0:M, 8 + h:9 + h], [M, M]), rhs=cummat_bcast_rhs, start=True, stop=True))
            for g in range(2):
                kb.op('pe', [gsel['g2']], [B3], lambda e: e.matmul(B3.f[0:M, g * 128:g * 128 + M], lhsT=gsel['g2'][:, 4 + g, 0:M], rhs=gsel['g2'][:, 6 + g, 0:M], start=True, stop=True))
            kb.op('dve', [B3], [GTm], lambda e: e.tensor_tensor(out=GTm[0:M, :, 0:M], in0=B3.f[0:M, 0:256].rearrange("p (a b) -> p a b", a=2)[:, :, 0:M], in1=bc(gmask.unsqueeze(1), [M, 2, M]), op=ALU.mult))
            if split:
                yield
            for h in range(8):
                bank = B4 if h < 4 else B5
                o = (h % 4) * 128
                kb.op('dve', [bank, dts], [F1], lambda e: e.tensor_scalar(out=F1[0:M, h, 0:M], in0=bank.f[0:M, o:o + M], scalar1=dts[0:M, 24 + h:25 + h], scalar2=0.0, op0=ALU.subtract, op1=ALU.min))
            kb.op('act', [F1], [G4], lambda e: e.activation(out=G4[0:M, :, 0:M], in_=F1[0:M, :, 0:M], func=AF.Exp))
            for g in range(2):
                kb.op('dve', [G4, GTm], [G4], lambda e: e.tensor_tensor(out=G4[0:M, 4 * g:4 * g + 4, 0:M], in0=G4[0:M, 4 * g:4 * g + 4, 0:M], in1=bc(GTm[0:M, g:g + 1, 0:M], [M, 4, M]), op=ALU.mult))

        def ssd_intra(M, cummat_bcast_rhs, gmask):
            for _ in ssd_intra_g(M, cummat_bcast_rhs, gmask, split=False):
                pass

        def front_F(j, xsrc):
            xsb = xs2[(j + NB_WARM) % 2]
            kb.load('sp', xsb, xsb[:], xsrc)
            norm_and_T(xsb, 128)

        def front_P_qkv(j):
            M = 128
            proj_tok(B1, 512, 0, M)
            proj_tok(B2, 256, 512, M)
            proj_tok(B2, 8, 2304, M, co=272)

        def front_P_xbc(j):
            M = 128
            proj_xbc(M, range(8), bks=(B6, B7))
            yield
            conv_silu(B6, B7, M, range(8), 1, cbuf, g2=G2s[j % 2])
            yield

        def front_P_z(j):
            M = 128
            proj_tok(B3, 512, 768, M)
            kb.op('act', [B3], [F2], lambda e: e.activation(out=F2[0:M, 0:4, :].rearrange("p a b -> p (a b)"), in_=B3.f[0:M, :], func=AF.Silu))

        def chain_Q(j, slot, last_of_core):
            M = 128
            par = j % 2
            rope_apply(B1, 0, 8, slot, M, None, None, qtm[0:M], qtm)
            yield
            for h in range(8):
                kb.op('pe', [qtm, ident], [T0], lambda e: e.transpose(out=T0.b[0:64, h * 128:h * 128 + M], in_=qtm[0:M, h, :], identity=ident[0:M, 0:M]), fin=(h == 7))
            kb.op('act', [T0], [qT], lambda e: e.copy(out=qT[0:64, :, 0:M], in_=T0.b[0:64, :].rearrange("p (a b) -> p a b", a=8)[:, :, 0:M]))
            yield
            kv_block(B2, slot, M, par, prm[:, 49:50].unsqueeze(2))
            if last_of_core:
                kb.store('sp', kf, kp_o, kf[:].rearrange("p h d -> p (h d)"))
                kb.store('sp', vf, vp_o, vf[:].rearrange("p h d -> p (h d)"))
            yield

        def stage_back_gen(j, blk_in_sg):
            xsb = xs2[(j + NB_WARM) % 2]
            return out_proj_gen(128, xsb, Hb[blk_in_sg][0:128, :], Hb[blk_in_sg], HnT[:, :, blk_in_sg * 128:blk_in_sg * 128 + 128], HnT, banks=(B4, B5), mixbuf=G7b)

        def run_merged(*gens):
            gens = [g for g in gens if g is not None]
            while gens:
                alive = []
                for g in gens:
                    try:
                        next(g)
                        alive.append(g)
                    except StopIteration:
                        pass
                gens = alive

        def stage_mid(j, blk_in_sg, slot, last_of_core, hook, backgen=None, q_done=False, early=None, prest=None):
            M = 128
            par = j % 2
            gsel['g2'] = G2s[j % 2]
            zs = F2[0:M, 0:4, :].rearrange("p a b -> p (a b)")

            def gen_D():
                dt_compute(B2, 272, M, None)
                yield
                ssd_state_update(M, trif[:, :], onesf[:, :], hT, hTb, bank=B2, split=1, sc=280)
                yield

            def gen_Q():
                if q_done:
                    return
                yield from chain_Q(j, slot, last_of_core)

            def gen_C():
                conv_shift(M)
                yield

            def gen_B1():
                if backgen is not None:
                    next(backgen)
                yield

            def gen_F():
                if hook is not None:
                    hook()
                yield

            if last_of_core:
                for _ in gen_D():
                    pass
                for half in range(2):
                    proj_tok(B6 if half == 0 else B7, 512, 1280 + half * 512, M)
                kb.op('act', [B6], [F1], lambda e: e.copy(out=xtv[0:M, 0:512], in_=B6.f[0:M, :]))
                kb.op('act', [B7], [F1], lambda e: e.copy(out=xtv[0:M, 512:1024], in_=B7.f[0:M, :]))
                kb.store('sp', F1, cp_o, xtv[125:128, :])
                run_merged(gen_B1(), gen_Q(), gen_C(), gen_F())
            else:
                run_merged(gen_B1(), gen_D(), gen_Q(), gen_C(), gen_F())

            def gen_A():
                sb = [B6, B7]
                i = 0
                for kvh in range(2):
                    for kbk, pp in ((0, 1 - par), (1, par)):
                        bank = sb[i % 2]; i += 1
                        kb.op('pe', [kT[pp], qT], [bank], lambda e: e.matmul(bank.f[:, 0:4 * M], lhsT=kT[pp][0:64, kvh, :], rhs=qT[0:64, 4 * kvh:4 * kvh + 4, 0:M], start=True, stop=True))
                        PTs = PTb[kbk * 2 + kvh]
                        kb.op('act', [bank], [PTs], lambda e: e.activation(out=PT[:, kbk, 4 * kvh:4 * kvh + 4, 0:M], in_=bank.f[:, 0:4 * M].rearrange("p (a b) -> p a b", a=4), func=AF.Exp, scale=0.125))
                        mk = maskp if kbk == 0 else maskd
                        kb.op('dve', [PTs, mk], [PTs], lambda e: e.tensor_tensor(out=PT[:, kbk, 4 * kvh:4 * kvh + 4, 0:M], in0=PT[:, kbk, 4 * kvh:4 * kvh + 4, 0:M], in1=bc(mk[:, 0:M].unsqueeze(1), [128, 4, M]), op=ALU.mult))
                    yield
                for kvh in range(2):
                    bank = B1 if kvh == 0 else B2
                    for r in range(4):
                        h = 4 * kvh + r
                        for kbk, pp in ((0, 1 - par), (1, par)):
                            kb.op('pe', [PTb[kbk * 2 + kvh], Va[pp]], [bank], lambda e: e.matmul(bank.f[0:M, r * 65:(r + 1) * 65], lhsT=PT[:, kbk, h, 0:M], rhs=Va[pp][:, kvh, :], start=(kbk == 0), stop=(kbk == 1)), fin=(kbk == 1 and r == 3))
                attn_finish(M)
                yield
                if early is not None:
                    yield from early()

            def gen_S():
                xs_B_tokmajor(M)
                yield
                ssd_state_update(M, trif[:, :], onesf[:, :], hT, hTb, split=2)
                xs3 = xsB[0:M, 0:512].rearrange("p (h d) -> p h d", h=8)
                kb.op('dve', [xsB, dts], [xdt], lambda e: e.tensor_tensor(out=xdt[0:M], in0=xs3, in1=bc(dts[0:M, 0:8].unsqueeze(2), [M, 8, 64]), op=ALU.mult))
                kb.op('pool', [xsB, prm], [xskip], lambda e: e.tensor_tensor(out=xskip[0:M], in0=xs3, in1=bc(prm[0:M, 16:24].unsqueeze(2), [M, 8, 64]), op=ALU.mult))
                yield
                yield from ssd_intra_g(M, trif[:, :], maskd[:, :], split=True)
                yield
                for hh in range(2):
                    bank = B4 if hh == 0 else B5
                    kb.op('act', [bank], [G5], lambda e: e.activation(out=G5[:, 4 * hh:4 * hh + 4, :], in_=bank.f[:, :].rearrange("p (a b) -> p a b", a=4), func=AF.Exp))
                for g in range(2):
                    kb.op('dve', [G5, gsel['g2']], [G5], lambda e: e.tensor_tensor(out=G5[:, 4 * g:4 * g + 4, :], in0=G5[:, 4 * g:4 * g + 4, :], in1=bc(gsel['g2'][:, 6 + g:7 + g, :], [128, 4, 128]), op=ALU.mult))
                yield
                for h in range(8):
                    kb.op('pe', [G4, xdt], [B3], lambda e: e.matmul(B3.f[0:M, h * 64:(h + 1) * 64], lhsT=G4[0:M, h, 0:M], rhs=xdt[0:M, h, :], start=True, stop=False), fin=False)
                    kb.op('pe', [G5, hTb], [B3], lambda e: e.matmul(B3.f[0:M, h * 64:(h + 1) * 64], lhsT=G5[:, h, 0:M], rhs=hTb[:, h, :], start=False, stop=True), fin=(h == 7))
                yield
                state_matmul_and_update(M, bank=B4)
                kb.op('act', [hT], [hTb], lambda e: e.copy(out=hTb[:], in_=hT[:]))
                yield
                if last_of_core:
                    for jt in range(4):
                        kb.op('pe', [hT, identf], [B5], lambda e: e.transpose(out=B5.f[:, jt * 128:(jt + 1) * 128], in_=hT[:].rearrange("p h d -> p (h d)")[:, jt * 128:(jt + 1) * 128], identity=identf[:, :]))
                    kb.op('act', [B5], [F1], lambda e: e.copy(out=xtv[:, 0:512], in_=B5.f[:, :]))
                    kb.store('sp', F1, sp_o.rearrange("(j p) n -> p j n", p=128), xtv[:, 0:512].rearrange("p (j n) -> p j n", j=4))
                ysf = tmpv[0:M].rearrange("p h d -> p (h d)")
                kb.op('dve', [B3, xskip], [F2], lambda e: e.tensor_tensor(out=ysf, in0=B3.f[0:M, :], in1=xskip[0:M].rearrange("p h d -> p (h d)"), op=ALU.add))
                if prest is not None:
                    prest()
                kb.op('dve', [F2, F2], [F2], lambda e: e.tensor_tensor(out=ysf, in0=ysf, in1=zs, op=ALU.mult))
                kb.op('act', [F2], [G6, st], lambda e: e.activation(out=G6[0:M, 512:1024], in_=ysf, func=AF.Square, accum_out=st[0:M, 4:5]))
                rstd_from_sumsq(4, 5, M, 512)
                kb.op('dve', [F2, st], [G6], lambda e: e.tensor_scalar(out=G6[0:M, 512:1024], in0=ysf, scalar1=st[0:M, 5:6], scalar2=None, op0=ALU.mult))
                yield

            run_merged(gen_A(), gen_S(), backgen)

        def out_proj_gen(M, xsb, hdst, hbuf, hnT_dst, hnTbuf, banks=None, mixbuf=None):
            banks = (B1, B2) if banks is None else banks
            G7_ = G7 if mixbuf is None else mixbuf
            transpose_to(G7_, G7_[:, :, 0:M], G6, G6[0:M, :], M, 8)
            for half in range(2):
                bank = banks[half]
                for kt in range(8):
                    kb.op('pe', [G7_, Wout], [bank], lambda e: e.matmul(bank.f[0:M, :], lhsT=G7_[:, kt, 0:M], rhs=Wout[:, kt, half * 512:(half + 1) * 512], start=(kt == 0), stop=(kt == 7)), fin=(kt == 7))
            yield
            for half in range(2):
                bank = banks[half]
                kb.op('dve', [bank, xsb], [hbuf], lambda e: e.tensor_tensor(out=hdst[:, half * 512:(half + 1) * 512], in0=bank.f[0:M, :], in1=xsb[0:M, half * 512:(half + 1) * 512], op=ALU.add))
            yield
            kb.op('act', [hbuf], [G1, st], lambda e: e.activation(out=G1[0:M, :], in_=hdst, func=AF.Square, accum_out=st[0:M, 6:7]))
            rstd_from_sumsq(6, 7, M, D)
            kb.op('dve', [hbuf, st], [G1], lambda e: e.tensor_scalar(out=G1[0:M, :], in0=hdst, scalar1=st[0:M, 7:8], scalar2=None, op0=ALU.mult))
            yield
            yield
            transpose_to(hnTbuf, hnT_dst, G1, G1[0:M, :], M, 8)
            yield

        def out_proj_and_store(*a, **k):
            for _ in out_proj_gen(*a, **k):
                pass

        qstate = {'n': 0}

        def mlp_group(wq, src_ap, ntok, nblk, hviews, hbuf, ui, do_down=True, mid=None):
            u = uT[ui % 2]
            wu, wd = wq
            for ft in range(8):
                bank = [B1, B2, B3, B4][ft % 4]
                for kt in range(8):
                    kb.op('pe', [wu, hnT_any], [bank], lambda e: e.matmul(bank.f[:, 0:ntok], lhsT=wu[:, kt, ft * 128:(ft + 1) * 128], rhs=src_ap[:, kt, :], start=(kt == 0), stop=(kt == 7)), fin=(kt == 7))
                r = ur[ft % 2]
                kb.op('act', [bank], [r], lambda e: e.activation(out=r[:, 0:ntok], in_=bank.f[:, 0:ntok], func=AF.Relu))
                kb.op('dve', [r], [u], lambda e: e.tensor_tensor(out=u[:, ft, 0:ntok], in0=r[:, 0:ntok], in1=r[:, 0:ntok], op=ALU.mult))
                if ft == 3 and mid is not None:
                    mid()
            if not do_down:
                return
            mlp_down(wq, hviews, ui)

        def mlp_down(wq, hviews, ui):
            u = uT[ui % 2]
            wu, wd = wq
            k = 0
            for bi, (M, hv, hbuf) in enumerate(hviews):
                for half in range(2):
                    bank = [B5, B6, B7, T0][k % 4]; k += 1
                    for ft in range(8):
                        kb.op('pe', [u, wd], [bank], lambda e: e.matmul(bank.f[0:M, :], lhsT=u[:, ft, bi * 128:bi * 128 + M], rhs=wd[:, ft, half * 512:(half + 1) * 512], start=(ft == 0), stop=(ft == 7)), fin=(ft == 7))
                    kb.op('dve', [bank, hbuf], [hbuf], lambda e: e.tensor_tensor(out=hv[:, half * 512:(half + 1) * 512], in0=hv[:, half * 512:(half + 1) * 512], in1=bank.f[0:M, :], op=ALU.add))

        def final_norm_store(M, hv, hbuf, dst, oi):
            assert ur[1].off == ur[0].off + ur[0].words
            junk = arena[:, ur[0].off:ur[0].off + 512].bitcast(BF16)[0:M, :]
            kb.op('act', [hbuf], [ur[0], ur[1], stB], lambda e: e.activation(out=junk, in_=hv, func=AF.Square, accum_out=stB[0:M, 0:1]))
            kb.op('act', [stB, prm], [stB], lambda e: e.activation(out=stB[0:M, 1:2], in_=stB[0:M, 0:1], func=AF.Ln, scale=1.0 / D, bias=epsb[0:M, :]))
            kb.op('act', [stB], [stB], lambda e: e.activation(out=stB[0:M, 1:2], in_=stB[0:M, 1:2], func=AF.Exp, scale=-0.5))
            kb.op('dve', [hbuf, stB, gfb], [hbuf], lambda e: e.scalar_tensor_tensor(out=hv, in0=hv, scalar=stB[0:M, 1:2], in1=gfb[0:M, :], op0=ALU.mult, op1=ALU.mult))
            kb.store('sp', hbuf, dst, hv)

        hnT_any = HnT

        def phase_b(sg, include_sample):
            kb.barrier()
            kb.load('sp', gfb, gfb[:], pbc(norm_f_g, D))
            items = []
            for c in range(4):
                for tg in range(SG // 4):
                    hv = [(128, Hb[tg * 4 + b][:, :], Hb[tg * 4 + b]) for b in range(4)]
                    items.append((c, HnT[:, :, tg * 512:(tg + 1) * 512], 512, hv, tg))
                if include_sample:
                    items.append((c, HnTs[:, :, :], 64, [(64, Hs[0:64, :], Hs)], -1))

            def finish_item(i):
                c, src, ntok, hv, tg = items[i]
                mlp_down(Wq[c % 2], hv, i)
                last_of_quarter = (i + 1 == len(items)) or (items[i + 1][0] != c)
                if last_of_quarter and c + 2 < 4:
                    load_quarter(Wq[c % 2], c + 2)
                if c == 3 and tg >= 0:
                    for b in range(4):
                        bb = tg * 4 + b
                        final_norm_store(128, Hb[bb][:, :], Hb[bb], y_f[(sg * SG + bb) * 128:(sg * SG + bb + 1) * 128, :], bb)

            load_quarter(Wq[1], 1)
            for i, (c, src, ntok, hv, tg) in enumerate(items):
                mlp_group(Wq[c % 2], src, ntok, len(hv), hv, None, i, do_down=False, mid=((lambda ii=i - 1: finish_item(ii)) if i > 0 else None))
            finish_item(len(items) - 1)
            if include_sample:
                final_norm_store(64, Hs[0:64, :], Hs, y_s, SG)
            kb.barrier()

        def sample_phase():
            M = 64
            SL = 17
            base = H.off
            c2 = [base]

            def salloc(name, shape, dt):
                b = alloc(name, shape, dt, at=c2[0]); c2[0] += b.words
                assert c2[0] <= regA, (name, c2[0], regA)
                return b
            S0s = [salloc("S0a", [128, 4, 4, 128], F32), salloc("S0b", [128, 4, 4, 128], F32)]
            h0Ts = [salloc("h0T%d" % i, [128, 512], BF16) for i in range(4)]
            S0h = salloc("S0h", [128, 4, 4, 128], BF16)
            ckb = salloc("ckb", [128, 16, 128], BF16)
            kTc = salloc("kTc", [128, 16, 2, 128], BF16)
            Vc = salloc("Vc", [128, 16, 2, 65], BF16)
            PcP = salloc("PcP", [128, 8, 16, 64], BF16)
            Pn = salloc("Pn", [128, 8, 64], BF16)
            CTp = salloc("CTp", [128, 2, 16, 64], BF16)
            Bmk = salloc("Bmk", [128, 16, 2, 128], BF16)
            cbs = salloc("cbs", [128, 8, 112], BF16)
            smask = salloc("smask", [128, 256], F32)
            kb.load('sp', smask, smask[:], smask_d)
            sctm = salloc("sctm", [128, D], F32)
            totT = salloc("totT", [128, 4, 16], F32)
            eac = salloc("eac", [128, 8], F32)
            yos = salloc("yos", [128, 8, 64], F32)
            kfs = salloc("kfs", [128, 2, 64], F32)
            bdf = smask[0:64, 0:64]
            seqm = smask[0:64, 64:80]
            cmask = smask[:, 80:84]
            bdb = salloc("bdb", [128, 64], BF16)
            cmb = salloc("cmb", [128, 4], BF16)
            kb.op('dve', [smask], [bdb], lambda e: e.tensor_copy(out=bdb[0:64, :], in_=bdf))
            kb.op('dve', [smask], [cmb], lambda e: e.tensor_copy(out=cmb[:, :], in_=cmask))

            xsb = xs2[0]
            kb.load('sp', xsb, xsb[0:M, :], xsm)
            kb.load('pool', ckb, ckb[:], cache_k.rearrange("b s c -> s b c"))
            kb.op('dve', [], [Vc], lambda e: e.memset(Vc[:], 1.0))
            for kvh in range(2):
                kb.load('pool', Vc, Vc[:, :, kvh, 0:64], cache_v[:, :, kvh * 64:(kvh + 1) * 64].rearrange("b s d -> s b d"))
            dd = Buf("dd", None)
            kb.load('sp', dd, ks_o[:, 0:124, :], cache_k[:, 4:128, :])
            kb.load('sp', dd, vs_o[:, 0:124, :], cache_v[:, 4:128, :])
            for jj in range(3):
                kb.load('sp', sctm, sctm[jj * 16:(jj + 1) * 16, :], state_conv[:, jj, :])
            for ct in range(8):
                bank = B4 if ct < 4 else B5
                o = (ct % 4) * 128
                kb.op('pe', [sctm, identf], [bank], lambda e: e.transpose(out=bank.f[:, o:o + 48], in_=sctm[0:48, ct * 128:(ct + 1) * 128], identity=identf[0:48, 0:48]))
            kb.op('act', [B4], [cbs], lambda e: e.copy(out=cbs[:, 0:4, 0:48], in_=B4.f[:, :].rearrange("p (a b) -> p a b", a=4)[:, :, 0:48]))
            kb.op('act', [B5], [cbs], lambda e: e.copy(out=cbs[:, 4:8, 0:48], in_=B5.f[:, :].rearrange("p (a b) -> p a b", a=4)[:, :, 0:48]))

            norm_and_T(xsb, M)
            proj_tok(B1, 512, 0, M)
            proj_tok(B2, 256, 512, M)
            proj_tok(B3, 512, 768, M)
            proj_xbc(M, range(8))
            rope_apply(B1, 0, 8, SL, M, None, None, qtm[0:M], qtm)
            for h in range(8):
                kb.op('pe', [qtm, ident], [T0], lambda e: e.transpose(out=T0.b[0:64, h * 128:h * 128 + M], in_=qtm[0:M, h, :], identity=ident[0:M, 0:M]))
            kb.op('act', [T0], [qT], lambda e: e.copy(out=qT[0:64, :, 0:M], in_=T0.b[0:64, :].rearrange("p (a b) -> p a b", a=8)[:, :, 0:M]))
            kv_block(B2, SL, M, 0, prm[:, 49:50].unsqueeze(2))
            for t in range(4):
                kb.store('sp', kf, ks_o[:, 124 + t, :], kf[16 * t:16 * t + 16].rearrange("p h d -> p (h d)"))
                kb.store('sp', vf, vs_o[:, 124 + t, :], vf[16 * t:16 * t + 16].rearrange("p h d -> p (h d)"))
            conv_silu(B4, B5, M, range(8), 16, cbs)
            for half in range(2):
                proj_tok(B6 if half == 0 else B7, 512, 1280 + half * 512, M)
            kb.op('act', [B6], [F1], lambda e: e.copy(out=xtv[0:M, 0:512], in_=B6.f[0:M, :]))
            kb.op('act', [B7], [F1], lambda e: e.copy(out=xtv[0:M, 512:1024], in_=B7.f[0:M, :]))
            for jj in range(3):
                kb.store('sp', F1, cs_o[:, jj, :], xtv[16 * (jj + 1):16 * (jj + 2), :])
            zs = F2[0:M, 0:4, :].rearrange("p a b -> p (a b)")
            kb.op('act', [B3], [F2], lambda e: e.activation(out=zs, in_=B3.f[0:M, :], func=AF.Silu))

            for b in range(16):
                for kvh in range(2):
                    kb.op('pe', [ckb, ident], [T0], lambda e: e.transpose(out=T0.b[0:64, ((b % 4) * 2 + kvh) * 128:((b % 4) * 2 + kvh + 1) * 128], in_=ckb[:, b, kvh * 64:(kvh + 1) * 64], identity=ident[:, :]))
                if b % 4 == 3:
                    b0 = b - 3
                    kb.op('act', [T0], [kTc], lambda e: e.copy(out=kTc[0:64, b0:b0 + 4, :, :], in_=T0.b[0:64, :].rearrange("p (a c d) -> p a c d", a=4, c=2)))
            for b in range(16):
                for kvh in range(2):
                    rhs = qT[0:64, 4 * kvh:4 * kvh + 4, b:64:16]
                    kb.op('pe', [kTc, qT], [B6], lambda e: e.matmul(B6.f[:, b * 32 + kvh * 16:b * 32 + kvh * 16 + 16], lhsT=kTc[0:64, b, kvh, :], rhs=rhs, start=True, stop=True))
            kb.op('dve', [], [PcP], lambda e: e.memset(PcP[:], 0.0))
            kb.op('act', [B6], [G4], lambda e: e.activation(out=G4[:, 0:4, :].rearrange("p a b -> p (a b)"), in_=B6.f[:, :], func=AF.Exp, scale=0.125))
            src = G4[:, 0:4, :].rearrange("p a b -> p (a b)").rearrange("p (b h t) -> p b h t", b=16, h=8)
            pc_t = PcP.t
            dst = bass.AP(pc_t.tensor, pc_t.offset, [list(pc_t.ap[0]), [65, 16], [1024, 8], [16, 4]])
            cm_t = cmb.t
            cmv = bass.AP(cm_t.tensor, cm_t.offset, [list(cm_t.ap[0]), [0, 16], [0, 8], [1, 4]])
            kb.op('dve', [G4, cmb], [PcP], lambda e: e.tensor_tensor(out=dst, in0=src, in1=cmv, op=ALU.mult))
            for kvh in range(2):
                kb.op('pe', [kT[0], qT], [B7], lambda e: e.matmul(B7.f[0:M, kvh * 256:(kvh + 1) * 256], lhsT=kT[0][0:64, kvh, 0:M], rhs=qT[0:64, 4 * kvh:4 * kvh + 4, 0:M], start=True, stop=True))
            kb.op('act', [B7], [Pn], lambda e: e.activation(out=Pn[0:M, :, :], in_=B7.f[0:M, :].rearrange("p (a b) -> p a b", a=8), func=AF.Exp, scale=0.125))
            kb.op('dve', [Pn, bdb], [Pn], lambda e: e.tensor_tensor(out=Pn[0:M, :, :], in0=Pn[0:M, :, :], in1=bc(bdb[0:M, :].unsqueeze(1), [M, 8, 64]), op=ALU.mult))
            for kvh in range(2):
                bank = B1 if kvh == 0 else B2
                for r in range(4):
                    h = 4 * kvh + r
                    for b in range(16):
                        kb.op('pe', [PcP, Vc], [bank], lambda e: e.matmul(bank.f[0:M, r * 65:(r + 1) * 65], lhsT=PcP[:, h, b, :], rhs=Vc[:, b, kvh, :], start=(b == 0), stop=False))
                    kb.op('pe', [Pn, Va[0]], [bank], lambda e: e.matmul(bank.f[0:M, r * 65:(r + 1) * 65], lhsT=Pn[0:M, h, :], rhs=Va[0][0:M, kvh, :], start=False, stop=True))
            attn_finish(M)

            xs_B_tokmajor(M)
            proj_tok(B6, 8, 2304, M)
            dt_compute(B6, 0, M, None)
            kb.op('pe', [dts, smask], [B2], lambda e: e.matmul(B2.f[0:M, 256:264], lhsT=bdf, rhs=dts[0:M, 8:16], start=True, stop=True))
            kb.op('act', [B2], [dts], lambda e: e.copy(out=dts[0:M, 24:32], in_=B2.f[0:M, 256:264]))
            kb.op('pe', [dts, smask], [B2], lambda e: e.matmul(B2.f[0:16, 272:280], lhsT=seqm, rhs=dts[0:M, 8:16], start=True, stop=True))
            kb.op('act', [B2], [eac], lambda e: e.copy(out=eac[0:16, :], in_=B2.f[0:16, 272:280]))
            seqmT = smask[0:16, 96:160]
            kb.op('pe', [eac, smask], [B2], lambda e: e.matmul(B2.f[0:M, 264:272], lhsT=seqmT, rhs=eac[0:16, :], start=True, stop=True))
            kb.op('act', [B2], [dts], lambda e: e.copy(out=dts[0:M, 32:40], in_=B2.f[0:M, 264:272]))
            kb.op('dve', [dts], [dts], lambda e: e.tensor_tensor(out=dts[0:M, 40:48], in0=dts[0:M, 32:40], in1=dts[0:M, 24:32], op=ALU.subtract))
            kb.op('act', [dts], [dts], lambda e: e.activation(out=dts[0:M, 40:48], in_=dts[0:M, 40:48], func=AF.Exp))
            kb.op('dve', [dts], [dts], lambda e: e.tensor_tensor(out=dts[0:M, 48:56], in0=dts[0:M, 40:48], in1=dts[0:M, 0:8], op=ALU.mult))
            xs3 = xsB[0:M, 0:512].rearrange("p (h d) -> p h d", h=8)
            kb.op('dve', [xsB, dts], [xdte], lambda e: e.tensor_tensor(out=xdte[0:M], in0=xs3, in1=bc(dts[0:M, 48:56].unsqueeze(2), [M, 8, 64]), op=ALU.mult))
            kb.op('dve', [xsB, dts], [xdt], lambda e: e.tensor_tensor(out=xdt[0:M], in0=xs3, in1=bc(dts[0:M, 0:8].unsqueeze(2), [M, 8, 64]), op=ALU.mult))
            kb.op('pool', [xsB, prm], [xskip], lambda e: e.tensor_tensor(out=xskip[0:M], in0=xs3, in1=bc(prm[0:M, 16:24].unsqueeze(2), [M, 8, 64]), op=ALU.mult))
            kb.op('act', [dts], [eac], lambda e: e.activation(out=eac[0:M, :], in_=dts[0:M, 24:32], func=AF.Exp))
            ssd_intra(M, bdf, bdb[0:M, :])
            for h in range(8):
                kb.op('pe', [G4, xdt], [B6], lambda e: e.matmul(B6.f[0:M, h * 64:(h + 1) * 64], lhsT=G4[0:M, h, 0:M], rhs=xdt[0:M, h, :], start=True, stop=True))
            kb.op('dve', [], [CTp], lambda e: e.memset(CTp[:], 0.0))
            ct_t = CTp.t
            dstc = bass.AP(ct_t.tensor, ct_t.offset, [list(ct_t.ap[0]), [1024, 2], [65, 16], [16, 4]])
            g2 = G2[:, 6:8, 0:64]
            srcc = bass.AP(g2.tensor, g2.offset, [list(g2.ap[0]), [128, 2], [1, 16], [16, 4]])
            kb.op('dve', [G2], [CTp], lambda e: e.tensor_copy(out=dstc, in_=srcc))
            b_t = xsB[0:M, 512:768]
            bsrc = bass.AP(b_t.tensor, b_t.offset, [list(b_t.ap[0]), [0, 16], [1, 256]])
            kb.op('dve', [xsB, smask], [Bmk], lambda e: e.tensor_tensor(out=Bmk[0:M].rearrange("p b g n -> p b (g n)"), in0=bsrc, in1=bc(seqm.unsqueeze(2), [M, 16, 256]), op=ALU.mult))
            kb.op('dve', [dts], [yos], lambda e: e.tensor_copy(out=yos[0:M], in_=bc(dts[0:M, 8:16].unsqueeze(2), [M, 8, 64])))
            for jt in range(4):
                lrep = yos[0:M].rearrange("p h d -> p (h d)")[:, jt * 128:(jt + 1) * 128]
                kb.op('pe', [yos, smask], [B7], lambda e: e.matmul(B7.f[:, jt * 16:(jt + 1) * 16], lhsT=lrep, rhs=seqm, start=True, stop=True))
            kb.op('act', [B7], [totT], lambda e: e.activation(out=totT[:].rearrange("p a b -> p (a b)"), in_=B7.f[:, 0:64], func=AF.Exp))
            kb.load('sp', S0s[0], S0s[0][:], state_ssm[0:4].rearrange("b (j p) n -> p b j n", p=128))
            for cch in range(4):
                S0 = S0s[cch % 2]
                if cch + 1 < 4:
                    Sn = S0s[(cch + 1) % 2]
                    kb.load('sp', Sn, Sn[:], state_ssm[(cch + 1) * 4:(cch + 2) * 4].rearrange("b (j p) n -> p b j n", p=128))
                kb.op('dve', [S0], [S0h], lambda e: e.tensor_copy(out=S0h[:], in_=S0[:]))
                for bl in range(4):
                    b = cch * 4 + bl
                    h0T = h0Ts[bl]
                    tbk = B3 if b % 2 == 0 else B2
                    sbk = B5 if b % 2 == 0 else B1
                    for jt in range(4):
                        kb.op('pe', [S0h, ident], [tbk], lambda e: e.transpose(out=tbk.b[:, jt * 128:(jt + 1) * 128], in_=S0h[:, bl, jt, :], identity=ident[:, :]), fin=(jt == 3))
                    kb.op('act', [tbk], [h0T], lambda e: e.copy(out=h0T[:, :], in_=tbk.b[:, 0:512]))
                    for g in range(2):
                        bk = B4 if g == 0 else B7
                        kb.op('pe', [CTp, h0T], [bk], lambda e: e.matmul(bk.f[0:M, 0:256], lhsT=CTp[:, g, b, :], rhs=h0T[:, g * 256:(g + 1) * 256], start=(b == 0), stop=(b == 15)))
                    for jt in range(4):
                        kb.op('pe', [xdte, Bmk], [sbk], lambda e: e.matmul(sbk.f[:, jt * 128:(jt + 1) * 128], lhsT=xdte[0:M].rearrange("p h d -> p (h d)")[:, jt * 128:(jt + 1) * 128], rhs=Bmk[0:M, b, jt // 2, :], start=True, stop=True), fin=(jt == 3))
                    for jt in range(4):
                        kb.op('dve', [S0, totT, sbk], [S0], lambda e: e.scalar_tensor_tensor(out=S0[:, bl, jt, :], in0=S0[:, bl, jt, :], scalar=totT[:, jt, b:b + 1], in1=sbk.f[:, jt * 128:(jt + 1) * 128], op0=ALU.mult, op1=ALU.add))
                kb.store('sp', S0, ss_o[cch * 4:(cch + 1) * 4].rearrange("b (j p) n -> p b j n", p=128), S0[:])
            for g in range(2):
                bk = B4 if g == 0 else B7
                kb.op('dve', [bk, eac], [yos], lambda e: e.tensor_tensor(out=yos[0:M, 4 * g:4 * g + 4, :], in0=bk.f[0:M, 0:256].rearrange("p (h d) -> p h d", h=4), in1=bc(eac[0:M, 4 * g:4 * g + 4].unsqueeze(2), [M, 4, 64]), op=ALU.mult))
            ysf = tmpv[0:M].rearrange("p h d -> p (h d)")
            kb.op('dve', [B6, yos], [F2], lambda e: e.tensor_tensor(out=ysf, in0=B6.f[0:M, :], in1=yos[0:M].rearrange("p h d -> p (h d)"), op=ALU.add))
            kb.op('dve', [F2, xskip], [F2], lambda e: e.tensor_tensor(out=ysf, in0=ysf, in1=xskip[0:M].rearrange("p h d -> p (h d)"), op=ALU.add))
            kb.op('dve', [F2, F2], [F2], lambda e: e.tensor_tensor(out=ysf, in0=ysf, in1=zs, op=ALU.mult))
            kb.op('act', [F2], [G6, st], lambda e: e.activation(out=G6[0:M, 512:1024], in_=ysf, func=AF.Square, accum_out=st[0:M, 4:5]))
            rstd_from_sumsq(4, 5, M, 512)
            kb.op('dve', [F2, st], [G6], lambda e: e.tensor_scalar(out=G6[0:M, 512:1024], in0=ysf, scalar1=st[0:M, 5:6], scalar2=None, op0=ALU.mult))
            dbg("s_ysb", F2, tmpv[0:M])
            dbg("s_yos", yos, yos[0:M])
            dbg("s_mix", G6, G6[0:M])
            out_proj_and_store(M, xsb, Hs[0:M, :], Hs, HnTs[:, :, 0:M], HnTs)

        if ENABLE_SAMPLE:
            kb.barrier()
            sample_phase()
            kb.barrier()
        for sg in range(NB_FULL // SG):
            load_quarter(Wq[0], 0)
            if sg > 0:
                build_dg()
            j0 = sg * SG
            front_F(j0, xf[j0 * 128:(j0 + 1) * 128, :])
            front_P_qkv(j0)
            for _ in front_P_xbc(j0):
                pass
            front_P_z(j0)
            backgen = None
            for b in range(SG):
                j = sg * SG + b
                nxt = (b + 1 < SG)
                hook = (lambda jj=j + 1: front_F(jj, xf[jj * 128:(jj + 1) * 128, :])) if nxt else None

                def early(jj=j + 1):
                    front_P_qkv(jj)
                    yield
                    yield from chain_Q(jj, jj, jj == NB_FULL - 1)
                    yield from front_P_xbc(jj)
                stage_mid(j, b, j, j == NB_FULL - 1, hook, backgen, q_done=(b > 0), early=(early if nxt else None), prest=None)
                backgen = stage_back_gen(j, b)
                next(backgen)
                if nxt:
                    front_P_z(j + 1)
            for _ in backgen:
                pass
            phase_b(sg, ENABLE_SAMPLE and sg == NB_FULL // SG - 1)
        kb.finish('sp')
    return nc


_CACHE = {}


def _rope_table(pos):
    half = 8
    inv = (np.float32(THETA) ** (-np.arange(half, dtype=np.float32) * np.float32(2.0) / np.float32(16))).astype(np.float32)
    ang = pos.astype(np.float32)[:, None] * inv[None, :]
    c = np.cos(ang).astype(np.float32); s = np.sin(ang).astype(np.float32)
    return np.concatenate([c, c, -s, s], axis=1)


def kernel(**inputs):
    f = lambda k: np.ascontiguousarray(np.asarray(inputs[k], dtype=np.float32))
    x_prompt = f('x_prompt'); x_sample = f('x_sample'); meta = f('meta_tokens')
    if 'nc' not in _CACHE:
        _CACHE['nc'] = build_program()
    nc = _CACHE['nc']
    smask = np.zeros((128, 256), np.float32)
    r = np.arange(64)
    tq = r // 16; bq = r % 16
    smask[0:64, 0:64] = ((bq[:, None] == bq[None, :]) & (tq[:, None] <= tq[None, :])).astype(np.float32)
    smask[0:64, 64:80] = (bq[:, None] == np.arange(16)[None, :]).astype(np.float32)
    smask[:, 80:84] = (np.arange(128)[:, None] > np.arange(4)[None, :]).astype(np.float32)
    smask[0:16, 96:160] = (np.arange(16)[:, None] == bq[None, :]).astype(np.float32)
    in_maps = []
    for c in range(8):
        seq, half = c // 2, c % 2
        xw = np.zeros((NB_WARM * 128, D), np.float32)
        rowmask = np.zeros((128, NB_WARM), np.float32)
        rope = np.zeros((128, 18, 32), np.float32)
        rr = np.arange(128)
        if half == 0:
            xw[16 * 128 + 112:17 * 128] = meta
            rowmask[112:, 16] = 1.0
            rope[:, 16, :] = _rope_table(np.maximum(rr - 112, 0))
            base = 16
            xfull = x_prompt[seq, 0:2048]
        else:
            xw[112:128] = meta
            rowmask[112:, 0] = 1.0
            xw[128:] = x_prompt[seq, 0:2048]
            rowmask[:, 1:] = 1.0
            rope[:, 16, :] = _rope_table(16 + 15 * 128 + rr)
            base = 16 + 2048
            xfull = x_prompt[seq, 2048:4096]
        for j in range(16):
            rope[:, j, :] = _rope_table(base + 128 * j + rr)
        rope[0:64, 17, :] = _rope_table(PAST + tq)
        xs_ = x_sample[16 * c:16 * c + 16]
        xsm = np.ascontiguousarray(xs_.transpose(1, 0, 2).reshape(64, D))
        m = {
            "xw": xw, "xf": np.ascontiguousarray(xfull), "xsm": xsm, "rowmask": rowmask, "rope": rope, "smask": smask,
            "w_in": f('w_in')[0], "w_out": f('w_out')[0], "w_up": f('w_up')[0], "w_down": f('w_down')[0],
            "norm1_g": f('norm1_g')[0], "norm2_g": f('norm2_g')[0], "norm_f_g": f('norm_f_g'),
            "attn_out_g": f('attn_out_g')[0], "ssm_out_g": f('ssm_out_g')[0], "attn_sinks": f('attn_sinks')[0],
            "dt_bias": f('dt_bias')[0], "a_log": f('a_log')[0], "d_skip": f('d_skip')[0],
            "conv_w": f('conv_w')[0], "conv_b": f('conv_b')[0],
            "cache_k": np.ascontiguousarray(f('cache_k')[0, 16 * c:16 * c + 16].reshape(16, 128, 128)),
            "cache_v": np.ascontiguousarray(f('cache_v')[0, 16 * c:16 * c + 16].reshape(16, 128, 128)),
            "state_conv": np.ascontiguousarray(f('state_conv')[0, 16 * c:16 * c + 16]),
            "state_ssm": np.ascontiguousarray(f('state_ssm')[0, 16 * c:16 * c + 16].reshape(16, 512, 128)),
        }
        in_maps.append(m)
    res = run_bass_kernel_spmd(nc, in_maps, core_ids=list(range(8)))
    R = res.results
    y_prompt = np.zeros((4, 4096, D), np.float32)
    y_sample = np.zeros((128, 4, D), np.float32)
    k_prompt = np.zeros((1, 4, 128, 2, 64), np.float32); v_prompt = np.zeros_like(k_prompt)
    conv_prompt = np.zeros((1, 4, 3, D), np.float32)
    ssm_prompt = np.zeros((1, 4, 8, 64, 128), np.float32)
    k_sample = np.zeros((1, 128, 128, 2, 64), np.float32); v_sample = np.zeros_like(k_sample)
    conv_sample = np.zeros((1, 128, 3, D), np.float32)
    ssm_sample = np.zeros((1, 128, 8, 64, 128), np.float32)
    for c in range(8):
        seq, half = c // 2, c % 2
        o = R[c]
        y_prompt[seq, half * 2048:(half + 1) * 2048] = o["y_f"]
        y_sample[16 * c:16 * c + 16] = o["y_s"].reshape(4, 16, D).transpose(1, 0, 2)
        if half == 1:
            k_prompt[0, seq] = o["kp_o"].reshape(128, 2, 64)
            v_prompt[0, seq] = o["vp_o"].reshape(128, 2, 64)
            conv_prompt[0, seq] = o["cp_o"]
            ssm_prompt[0, seq] = o["sp_o"].reshape(8, 64, 128)
        k_sample[0, 16 * c:16 * c + 16] = o["ks_o"].reshape(16, 128, 2, 64)
        v_sample[0, 16 * c:16 * c + 16] = o["vs_o"].reshape(16, 128, 2, 64)
        conv_sample[0, 16 * c:16 * c + 16] = o["cs_o"]
        ssm_sample[0, 16 * c:16 * c + 16] = o["ss_o"].reshape(16, 8, 64, 128)
    return (y_prompt, y_sample, k_prompt, v_prompt, conv_prompt, ssm_prompt,
            k_sample, v_sample, conv_sample, ssm_sample)
```

```python
import contextlib
import numpy as np
import concourse.bass as bass
import concourse.mybir as mybir
from concourse.bass_utils import run_bass_kernel_spmd

F32 = mybir.dt.float32
BF16 = mybir.dt.bfloat16
AF = mybir.ActivationFunctionType
ALU = mybir.AluOpType

D = 1024
NIN = 2312
NB_FULL = 16
NB_WARM = 17
SG = 8
PAST = 16384
THETA = 500000.0
EPS = 1e-6
ENABLE_SAMPLE = True
DEBUG = False
DBG_OUT = []


class Buf:
    def __init__(self, name, ap):
        self.name = name; self.t = ap; self.w = None; self.r = {}
        self.dsem = None; self.dcnt = 0; self.rsem = None; self.rcnt = 0

    def __getitem__(self, k):
        return self.t[k]


class KB:
    def __init__(self, nc, es):
        self.nc = nc; self.es = es
        self.eng = {'pe': nc.tensor, 'act': nc.scalar, 'dve': nc.vector, 'pool': nc.gpsimd, 'sp': nc.sync}
        self.sem = {e: es.enter_context(nc.semaphore("sem_" + e)) for e in self.eng}
        self.cnt = {e: 0 for e in self.eng}
        self.known = {e: {} for e in self.eng}
        self.dma_toks = {}

    def newsem(self, name):
        return self.es.enter_context(self.nc.semaphore(name))

    def _wait(self, e, tok):
        sem, val = tok
        if e == 'pe' and sem is self.sem['pe']:
            return
        k = self.known[e]
        if k.get(id(sem), 0) >= val:
            return
        k[id(sem)] = val
        self.eng[e].wait_ge(sem, val)

    def _deps(self, e, reads, writes):
        for b in reads:
            if b.w: self._wait(e, b.w)
        for b in writes:
            if b.w: self._wait(e, b.w)
            for t in b.r.values(): self._wait(e, t)

    def op(self, e, reads, writes, fn, fin=True):
        self._deps(e, reads, writes)
        ins = fn(self.eng[e])
        if fin or e != 'pe':
            self.cnt[e] += 1
            ins.then_inc(self.sem[e], 1)
            tok = (self.sem[e], self.cnt[e])
            self.pe_pending = False if e == 'pe' else getattr(self, 'pe_pending', False)
        else:
            tok = (self.sem[e], self.cnt[e] + 1)
            self.pe_pending = True
        for b in writes:
            b.w = tok; b.r = {}
        for b in reads:
            if b not in writes: b.r[e] = tok
        return ins

    def load(self, q, buf, out_ap, in_ap, **kw):
        self._deps(q, [], [buf])
        if buf.dsem is None: buf.dsem = self.newsem("d_" + buf.name)
        ins = self.eng[q].dma_start(out=out_ap, in_=in_ap, **kw)
        buf.dcnt += 16
        ins.then_inc(buf.dsem, 16)
        buf.w = (buf.dsem, buf.dcnt); buf.r = {}
        self.dma_toks[id(buf.dsem)] = buf.w

    def store(self, q, buf, out_ap, in_ap, **kw):
        self._deps(q, [buf], [])
        if buf.rsem is None: buf.rsem = self.newsem("r_" + buf.name)
        ins = self.eng[q].dma_start(out=out_ap, in_=in_ap, **kw)
        buf.rcnt += 16
        ins.then_inc(buf.rsem, 16)
        buf.r['st'] = (buf.rsem, buf.rcnt)
        self.dma_toks[id(buf.rsem)] = buf.r['st']

    def barrier(self):
        assert not getattr(self, 'pe_pending', False)
        for e in self.eng:
            for f in self.eng:
                if f != e and self.cnt[f] > 0:
                    self._wait(e, (self.sem[f], self.cnt[f]))
            for t in self.dma_toks.values():
                self._wait(e, t)

    def finish(self, q):
        for t in self.dma_toks.values():
            self._wait(q, t)


def bc(ap, shape):
    return ap.to_broadcast(list(shape))


def build_program():
    nc = bass.Bass("TRN2", target_bir_lowering=False)

    def din(name, shape):
        return nc.dram_tensor(name, list(shape), F32, kind="ExternalInput").ap()

    def dout(name, shape):
        return nc.dram_tensor(name, list(shape), F32, kind="ExternalOutput").ap()

    xw = din("xw", [NB_WARM * 128, D])
    xf = din("xf", [NB_FULL * 128, D])
    xsm = din("xsm", [64, D])
    rowmask_d = din("rowmask", [128, NB_WARM])
    rope_d = din("rope", [128, 18, 32])
    smask_d = din("smask", [128, 256])
    w_in = din("w_in", [D, NIN]); w_out = din("w_out", [D, D])
    w_up = din("w_up", [D, 4096]); w_down = din("w_down", [4096, D])
    norm1_g = din("norm1_g", [D]); norm2_g = din("norm2_g", [D]); norm_f_g = din("norm_f_g", [D])
    attn_out_g = din("attn_out_g", [512]); ssm_out_g = din("ssm_out_g", [512])
    attn_sinks = din("attn_sinks", [8]); dt_bias = din("dt_bias", [8]); a_log = din("a_log", [8]); d_skip = din("d_skip", [8])
    conv_w = din("conv_w", [4, D]); conv_b = din("conv_b", [D])
    cache_k = din("cache_k", [16, 128, 128]); cache_v = din("cache_v", [16, 128, 128])
    state_conv = din("state_conv", [16, 3, D]); state_ssm = din("state_ssm", [16, 512, 128])

    y_f = dout("y_f", [NB_FULL * 128, D]); y_s = dout("y_s", [64, D])
    kp_o = dout("kp_o", [128, 128]); vp_o = dout("vp_o", [128, 128])
    cp_o = dout("cp_o", [3, D]); sp_o = dout("sp_o", [512, 128])
    ks_o = dout("ks_o", [16, 128, 128]); vs_o = dout("vs_o", [16, 128, 128])
    cs_o = dout("cs_o", [16, 3, D]); ss_o = dout("ss_o", [16, 512, 128])

    es = contextlib.ExitStack()
    with es:
        kb = KB(nc, es)
        AW = 53200
        arena = es.enter_context(nc.sbuf_tensor("arena", [128, AW], F32))
        cur = [0]

        def alloc(name, shape, dt, at=None):
            n = int(np.prod(shape[1:]))
            words = n if dt == F32 else (n + 1) // 2
            words = (words + 7) // 8 * 8
            if at is None:
                off = cur[0]; cur[0] += words
            else:
                off = at
            assert off + words <= AW, (name, off, words)
            ap = arena[:, off:off + words]
            if dt == BF16:
                ap = ap.bitcast(BF16)
            ap = ap[:, 0:n]
            if len(shape) == 3:
                ap = ap.rearrange("p (a b) -> p a b", a=shape[1])
            elif len(shape) == 4:
                ap = ap.rearrange("p (a b c) -> p a b c", a=shape[1], b=shape[2])
            b = Buf(name, ap); b.off = off; b.words = words
            return b

        banks = []
        for i in range(8):
            t = es.enter_context(nc.psum_tensor("bank%d" % i, [128, 512], F32))
            b = Buf("bank%d" % i, t)
            b.f = t[:, :]
            b.b = t[:, :].bitcast(BF16)
            banks.append(b)
        T0, B1, B2, B3, B4, B5, B6, B7 = banks

        ident = alloc("ident", [128, 128], BF16)
        identf = alloc("identf", [128, 128], F32)
        trif = alloc("trif", [128, 128], F32)
        onesf = alloc("onesf", [128, 128], F32)
        maskd = alloc("maskd", [128, 128], BF16)
        maskp = alloc("maskp", [128, 128], BF16)
        rope = alloc("rope", [128, 18, 32], F32)
        rowmask = alloc("rowmask", [128, NB_WARM], F32)
        cbrow = alloc("cbrow", [1, 1024], BF16)
        onesrow = alloc("onesrow", [1, 512], BF16)
        prm = alloc("prm", [128, 64], F32)
        cw = alloc("cw", [128, 4, 8], F32)
        cb = alloc("cb", [128, 8], F32)
        gsc = alloc("gsc", [128, 24], F32)
        Hs = alloc("Hs", [128, D], F32)
        HnTs = alloc("HnTs", [128, 8, 64], BF16)
        cbuf = alloc("cbuf", [128, 8, 131], BF16)
        kT = [alloc("kT%d" % i, [128, 2, 128], BF16) for i in range(2)]
        Va = [alloc("Va%d" % i, [128, 2, 65], BF16) for i in range(2)]
        hT = alloc("hT", [128, 8, 64], F32)
        hTb = alloc("hTb", [128, 8, 64], BF16)
        Win = alloc("Win", [128, 8, NIN], BF16)
        Wout = alloc("Wout", [128, 8, D], BF16)
        H = alloc("H", [128, SG, D], F32)
        HnT = alloc("HnT", [128, 8, SG * 128], BF16)
        Hb = [Buf('H%d' % i, H.t[:, i, :]) for i in range(SG)]
        regX = cur[0]
        Wq0u = alloc("Wq0u", [128, 8, 1024], BF16)
        Wq0d = alloc("Wq0d", [128, 8, 1024], BF16)
        regA = cur[0]

        xs2 = [alloc("xs%d" % i, [128, D], F32) for i in range(2)]
        G1 = alloc("G1", [128, D], BF16)
        G2 = alloc("G2", [128, 8, 128], BF16)
        G2b = alloc("G2b", [128, 8, 128], BF16)
        G2s = [G2, G2b]
        gsel = {"g2": G2}
        G4 = alloc("G4", [128, 8, 128], BF16)
        G5 = alloc("G5", [128, 8, 128], BF16)
        G6 = alloc("G6", [128, D], BF16)
        G7 = alloc("G7", [128, 8, 128], BF16)
        G7b = alloc("G7b", [128, 8, 128], BF16)
        F1 = alloc("F1", [128, 8, 128], F32)
        Dg = alloc("Dg", [128, 8, 4, 128], BF16)
        F2 = alloc("F2", [128, 8, 128], F32)
        st = alloc("st", [128, 32], F32)
        qtm = alloc("qtm", [128, 8, 64], BF16)
        ta = alloc("ta", [128, 8, 16], F32)
        tb = alloc("tb", [128, 8, 16], F32)
        qT = alloc("qT", [128, 8, 128], BF16)
        kf = alloc("kf", [128, 2, 64], F32)
        vf = alloc("vf", [128, 2, 64], F32)
        ktm = alloc("ktm", [128, 2, 64], BF16)
        PT = alloc("PT", [128, 2, 8, 128], BF16)
        PTb = [Buf("PTs%d" % i, PT.t) for i in range(4)]
        xsB = alloc("xsB", [128, 768], BF16)
        xdt = alloc("xdt", [128, 8, 64], BF16)
        xdte = alloc("xdte", [128, 8, 64], BF16)
        xskip = alloc("xskip", [128, 8, 64], BF16)
        dts = alloc("dts", [128, 64], F32)
        GTm = alloc("GTm", [128, 2, 128], BF16)
        tmpv = F2.t[:, 4:8, :].rearrange("p a (b c) -> p (a b) c", c=64)
        xtv = F1.t.rearrange("p a b -> p (a b)")
        endA = cur[0]
        cur[0] = regA
        Wq1u = alloc("Wq1u", [128, 8, 1024], BF16)
        Wq1d = alloc("Wq1d", [128, 8, 1024], BF16)
        uT = [alloc("uT%d" % i, [128, 8, 512], BF16) for i in range(2)]
        ur = [alloc("ur%d" % i, [128, 512], BF16) for i in range(2)]
        gfb = alloc("gfb", [128, D], F32)
        stB = alloc("stB", [128, 8], F32)
        endB = cur[0]
        assert max(endA, endB) <= AW, (endA, endB, AW)
        Wq = [(Wq0u, Wq0d), (Wq1u, Wq1d)]

        def dbg(name, buf, ap, q='sp'):
            if not DEBUG:
                return
            dt = ap.dtype
            t = nc.dram_tensor("dbg_" + name, list(ap.shape), dt, kind="ExternalOutput").ap()
            DBG_OUT.append("dbg_" + name)
            kb.store(q, buf, t, ap)

        def pool_mask(buf, pattern, cmp, cm, base=0, fill=0.0):
            kb.op('pool', [buf], [buf], lambda e: e.affine_select(out=buf[:], in_=buf[:], pattern=pattern, compare_op=cmp, fill=fill, base=base, channel_multiplier=cm))

        kb.op('pool', [], [identf], lambda e: e.memset(identf[:], 1.0))
        pool_mask(identf, [[-1, 128]], ALU.is_equal, 1)
        kb.op('pool', [], [trif], lambda e: e.memset(trif[:], 1.0))
        pool_mask(trif, [[1, 128]], ALU.is_ge, -1)
        kb.op('pool', [], [onesf], lambda e: e.memset(onesf[:], 1.0))
        kb.op('dve', [identf], [ident], lambda e: e.tensor_copy(out=ident[:], in_=identf[:]))
        kb.op('dve', [trif], [maskd], lambda e: e.tensor_copy(out=maskd[:], in_=trif[:]))
        kb.op('dve', [trif], [maskp], lambda e: e.tensor_scalar(out=maskp[:], in0=trif[:], scalar1=-1.0, scalar2=1.0, op0=ALU.mult, op1=ALU.add))
        kb.load('sp', rope, rope[:], rope_d)
        kb.load('sp', rowmask, rowmask[:], rowmask_d)
        kb.load('pool', cbrow, cbrow[0:1, :], bass.AP(conv_b.tensor, conv_b.offset, [[0, 1], [1, 1024]]))
        kb.op('dve', [], [onesrow], lambda e: e.memset(onesrow[:], 1.0))

        def pbc(v, n):
            return bass.AP(v.tensor, v.offset, [[0, 128], [1, n]])
        kb.load('sp', prm, prm[:, 0:8], pbc(dt_bias, 8))
        kb.load('sp', prm, prm[:, 8:16], pbc(a_log, 8))
        kb.load('sp', prm, prm[:, 16:24], pbc(d_skip, 8))
        kb.load('sp', prm, prm[:, 24:32], pbc(attn_sinks, 8))
        for jj in range(4):
            kb.load('sp', cw, cw[:, jj, :], conv_w[jj].rearrange("(ct p) -> p ct", p=128), allow_slow_non_contiguous=True)
        kb.load('sp', cb, cb[:], conv_b.rearrange("(ct p) -> p ct", p=128), allow_slow_non_contiguous=True)
        kb.load('sp', gsc, gsc[:, 0:8], norm1_g.rearrange("(kt p) -> p kt", p=128), allow_slow_non_contiguous=True)
        kb.load('sp', gsc, gsc[:, 8:12], attn_out_g.rearrange("(kt p) -> p kt", p=128), allow_slow_non_contiguous=True)
        kb.load('sp', gsc, gsc[:, 12:16], ssm_out_g.rearrange("(kt p) -> p kt", p=128), allow_slow_non_contiguous=True)
        kb.load('sp', gsc, gsc[:, 16:24], norm2_g.rearrange("(kt p) -> p kt", p=128), allow_slow_non_contiguous=True)
        kb.op('dve', [prm], [prm], lambda e: e.memset(prm[:, 48:49], EPS))
        kb.op('dve', [prm], [prm], lambda e: e.memset(prm[:, 49:50], 1.0))
        kb.op('act', [prm], [prm], lambda e: e.activation(out=prm[:, 32:40], in_=prm[:, 8:16], func=AF.Exp))
        kb.op('dve', [prm], [prm], lambda e: e.tensor_scalar(out=prm[:, 32:40], in0=prm[:, 32:40], scalar1=-1.0, scalar2=None, op0=ALU.mult))
        kb.op('act', [prm], [prm], lambda e: e.activation(out=prm[:, 40:48], in_=prm[:, 24:32], func=AF.Exp))
        epsb = prm[:, 48:49]

        w_in_v = w_in.rearrange("(kt p) n -> p kt n", p=128)
        for (c0, c1) in [(1280, 2312), (0, 640), (640, 1280)]:
            kb.load('pool', Win, Win[:, :, c0:c1], w_in_v[:, :, c0:c1])
        kb.load('pool', Wout, Wout[:], w_out.rearrange("(kt p) n -> p kt n", p=128))
        for kt in range(8):
            kb.op('dve', [Win, gsc], [Win], lambda e: e.tensor_scalar(out=Win[:, kt, :], in0=Win[:, kt, :], scalar1=gsc[:, kt:kt + 1], scalar2=None, op0=ALU.mult))
        for kt in range(8):
            kb.op('dve', [Wout, gsc], [Wout], lambda e: e.tensor_scalar(out=Wout[:, kt, :], in0=Wout[:, kt, :], scalar1=gsc[:, 8 + kt:9 + kt], scalar2=None, op0=ALU.mult))

        kb.barrier()
        w_up_v = w_up.rearrange("(kt p) f -> p kt f", p=128)
        w_dn_v = w_down.rearrange("(ft p) d -> p ft d", p=128)

        def load_quarter(bufs, c):
            bu, bd = bufs
            kb.load('pool', bu, bu[:], w_up_v[:, :, c * 1024:(c + 1) * 1024])
            kb.load('pool', bd, bd[:], w_dn_v[:, c * 8:(c + 1) * 8, :])
            for kt in range(8):
                kb.op('dve', [bu, gsc], [bu], lambda e: e.tensor_scalar(out=bu[:, kt, :], in0=bu[:, kt, :], scalar1=gsc[:, 16 + kt:17 + kt], scalar2=None, op0=ALU.mult))

        def rstd_from_sumsq(col_in, col_out, M, n):
            kb.op('act', [st, prm], [st], lambda e: e.activation(out=st[0:M, col_out:col_out + 1], in_=st[0:M, col_in:col_in + 1], func=AF.Ln, scale=1.0 / n, bias=epsb[0:M, :]))
            kb.op('act', [st], [st], lambda e: e.activation(out=st[0:M, col_out:col_out + 1], in_=st[0:M, col_out:col_out + 1], func=AF.Exp, scale=-0.5))

        def transpose_to(dst_buf, dst_ap3, src_buf, src2d, M, nkt):
            for kt in range(nkt):
                kb.op('pe', [src_buf, ident], [T0], lambda e: e.transpose(out=T0.b[:, kt * 128:kt * 128 + M], in_=src2d[:, kt * 128:(kt + 1) * 128], identity=ident[0:M, 0:M]))
            kb.op('act', [T0], [dst_buf], lambda e: e.copy(out=dst_ap3, in_=T0.b[:, 0:nkt * 128].rearrange("p (a b) -> p a b", a=nkt)[:, :, 0:M]))

        def norm_and_T(xsb, M):
            kb.op('act', [xsb], [G1, st], lambda e: e.activation(out=G1[0:M, :], in_=xsb[0:M, :], func=AF.Square, accum_out=st[0:M, 0:1]))
            rstd_from_sumsq(0, 1, M, D)
            kb.op('dve', [xsb, st], [G1], lambda e: e.tensor_scalar(out=G1[0:M, :], in0=xsb[0:M, :], scalar1=st[0:M, 1:2], scalar2=None, op0=ALU.mult))
            transpose_to(G7, G7[:, :, 0:M], G1, G1[0:M, :], M, 8)

        def proj_tok(bank, ncols, c0, M, co=0):
            for kt in range(8):
                kb.op('pe', [G7, Win], [bank], lambda e: e.matmul(bank.f[0:M, co:co + ncols], lhsT=G7[:, kt, 0:M], rhs=Win[:, kt, c0:c0 + ncols], start=(kt == 0), stop=(kt == 7)), fin=(kt == 7))

        def proj_xbc(M, cts, bks=None):
            bks = (B4, B5) if bks is None else bks
            for ct in cts:
                bank = bks[0] if ct < 4 else bks[1]
                o = (ct % 4) * 128
                for kt in range(8):
                    kb.op('pe', [G7, Win], [bank], lambda e: e.matmul(bank.f[:, o:o + M], lhsT=Win[:, kt, 1280 + ct * 128:1280 + (ct + 1) * 128], rhs=G7[:, kt, 0:M], start=(kt == 0), stop=(kt == 7)), fin=(kt == 7))

        def rope_apply(bank, c0, nh, slot, M, dst32, dst32buf, dst16, dst16buf):
            src = bank.f[0:M, c0:c0 + nh * 64].rearrange("p (h d) -> p h d", h=nh)
            cc = bc(rope[0:M, slot:slot + 1, 0:16], [M, nh, 16])
            ns = bc(rope[0:M, slot:slot + 1, 16:24], [M, nh, 8])
            ps = bc(rope[0:M, slot:slot + 1, 24:32], [M, nh, 8])
            kb.op('dve', [bank, rope], [ta], lambda e: e.tensor_tensor(out=ta[0:M, 0:nh, :], in0=src[:, :, 0:16], in1=cc, op=ALU.mult))
            kb.op('dve', [bank, rope], [tb], lambda e: e.tensor_tensor(out=tb[0:M, 0:nh, 0:8], in0=src[:, :, 8:16], in1=ns, op=ALU.mult))
            kb.op('dve', [bank, rope], [tb], lambda e: e.tensor_tensor(out=tb[0:M, 0:nh, 8:16], in0=src[:, :, 0:8], in1=ps, op=ALU.mult))
            if dst32 is not None:
                kb.op('dve', [ta, tb], [dst32buf], lambda e: e.tensor_tensor(out=dst32[:, :, 0:16], in0=ta[0:M, 0:nh, :], in1=tb[0:M, 0:nh, :], op=ALU.add))
                kb.op('act', [bank], [dst32buf], lambda e: e.copy(out=dst32[:, :, 16:64], in_=src[:, :, 16:64]))
                kb.op('act', [dst32buf], [dst16buf], lambda e: e.copy(out=dst16, in_=dst32))
            else:
                kb.op('dve', [ta, tb], [dst16buf], lambda e: e.tensor_tensor(out=dst16[:, :, 0:16], in0=ta[0:M, 0:nh, :], in1=tb[0:M, 0:nh, :], op=ALU.add))
                kb.op('act', [bank], [dst16buf], lambda e: e.copy(out=dst16[:, :, 16:64], in_=src[:, :, 16:64]))

        def kv_block(bank, slot, M, par, onescol):
            rope_apply(bank, 0, 2, slot, M, kf[0:M], kf, ktm[0:M], ktm)
            kb.op('act', [bank], [vf], lambda e: e.copy(out=vf[0:M], in_=bank.f[0:M, 128:256].rearrange("p (h d) -> p h d", h=2)))
            kb.op('dve', [vf], [Va[par]], lambda e: e.tensor_copy(out=Va[par][0:M, :, 0:64], in_=vf[0:M]))
            kb.op('dve', [rowmask, prm], [Va[par]], lambda e: e.tensor_copy(out=Va[par][0:M, :, 64:65], in_=bc(onescol[0:M], [M, 2, 1])))
            for h in range(2):
                kb.op('pe', [ktm, ident], [T0], lambda e: e.transpose(out=T0.b[0:64, h * 128:h * 128 + M], in_=ktm[0:M, h, :], identity=ident[0:M, 0:M]))
            kb.op('act', [T0], [kT[par]], lambda e: e.copy(out=kT[par][0:64, :, 0:M], in_=T0.b[0:64, 0:256].rearrange("p (a b) -> p a b", a=2)[:, :, 0:M]))

        def build_dg():
            for ct in range(8):
                for j in range(4):
                    kb.op('dve', [ident, cw], [Dg], lambda e: e.tensor_scalar(out=Dg[:, ct, j, :], in0=ident[:, :], scalar1=cw[:, j, ct:ct + 1], scalar2=None, op0=ALU.mult))

        def conv_silu(bankA, bankB, M, cts, shift, cbw, g2=None):
            G2_ = gsel['g2'] if g2 is None else g2
            off = 3 * shift
            lo, hi = min(cts), max(cts) + 1
            if lo < 4:
                kb.op('act', [bankA], [cbw], lambda e: e.copy(out=cbw[:, lo:min(hi, 4), off:off + M], in_=bankA.f[:, :].rearrange("p (a b) -> p a b", a=4)[:, lo:min(hi, 4), 0:M]))
            if hi > 4:
                kb.op('dve', [bankB], [cbw], lambda e: e.tensor_copy(out=cbw[:, 4:hi, off:off + M], in_=bankB.f[:, :].rearrange("p (a b) -> p a b", a=4)[:, 0:hi - 4, 0:M]))
            for ct in range(lo, hi):
                bank = bankA if ct < 4 else bankB
                o = (ct % 4) * 128
                for j in range(4):
                    kb.op('pe', [Dg, cbw], [bank], lambda e: e.matmul(bank.f[:, o:o + M], lhsT=Dg[:, ct, j, :], rhs=cbw[:, ct, j * shift:j * shift + M], start=(j == 0), stop=False), fin=False)
                kb.op('pe', [cbrow, onesrow], [bank], lambda e: e.matmul(bank.f[:, o:o + M], lhsT=cbrow[0:1, ct * 128:(ct + 1) * 128], rhs=onesrow[0:1, 0:M], start=False, stop=True), fin=(ct == hi - 1 or ct == 3))
            halves = []
            if lo < 4: halves.append((bankA, lo, min(hi, 4), 0))
            if hi > 4: halves.append((bankB, 4, hi, 4))
            for bank, a, b_, base in halves:
                kb.op('act', [bank], [G2_], lambda e: e.activation(out=G2_[:, a:b_, 0:M], in_=bank.f[:, :].rearrange("p (a b) -> p a b", a=4)[:, a - base:b_ - base, 0:M], func=AF.Silu))

        def conv_shift(M):
            kb.op('pool', [cbuf], [cbuf], lambda e: e.tensor_copy(out=cbuf[:, :, 0:3], in_=cbuf[:, :, M:M + 3]))

        def xs_B_tokmajor(M):
            for ct in range(6):
                kb.op('pe', [gsel['g2'], ident], [T0], lambda e: e.transpose(out=T0.b[0:M, ct * 128:(ct + 1) * 128], in_=gsel['g2'][:, ct, 0:M], identity=ident[:, :]), fin=(ct == 5))
            kb.op('act', [T0], [xsB], lambda e: e.copy(out=xsB[0:M, :], in_=T0.b[0:M, 0:768]))

        def dt_compute(bank, c0, M, maskcol):
            kb.op('dve', [bank, prm], [dts], lambda e: e.tensor_tensor(out=dts[0:M, 0:8], in0=bank.f[0:M, c0:c0 + 8], in1=prm[0:M, 0:8], op=ALU.add))
            kb.op('act', [dts], [dts], lambda e: e.activation(out=dts[0:M, 0:8], in_=dts[0:M, 0:8], func=AF.Exp))
            kb.op('act', [dts, prm], [dts], lambda e: e.activation(out=dts[0:M, 0:8], in_=dts[0:M, 0:8], func=AF.Ln, bias=prm[0:M, 49:50]))
            if maskcol is not None:
                kb.op('dve', [dts, rowmask], [dts], lambda e: e.tensor_scalar(out=dts[0:M, 0:8], in0=dts[0:M, 0:8], scalar1=maskcol, scalar2=None, op0=ALU.mult))
            kb.op('dve', [dts, prm], [dts], lambda e: e.tensor_tensor(out=dts[0:M, 8:16], in0=dts[0:M, 0:8], in1=prm[0:M, 32:40], op=ALU.mult))

        def ssd_state_update(M, cummat, totmat, hTl, hTbl, bank=None, split=0, sc=256):
            B2_ = B2 if bank is None else bank
            if split in (0, 1):
                kb.op('pe', [dts], [B2_], lambda e: e.matmul(B2_.f[0:M, sc:sc + 8], lhsT=cummat, rhs=dts[0:M, 8:16], start=True, stop=True))
                kb.op('pe', [dts], [B2_], lambda e: e.matmul(B2_.f[0:128, sc + 8:sc + 16], lhsT=totmat, rhs=dts[0:M, 8:16], start=True, stop=True))
                kb.op('act', [B2_], [dts], lambda e: e.copy(out=dts[0:M, 24:32], in_=B2_.f[0:M, sc:sc + 8]))
                kb.op('act', [B2_], [dts], lambda e: e.copy(out=dts[:, 32:40], in_=B2_.f[:, sc + 8:sc + 16]))
                kb.op('act', [B2_], [dts], lambda e: e.activation(out=dts[:, 56:64], in_=B2_.f[:, sc + 8:sc + 16], func=AF.Exp))
                kb.op('dve', [dts], [dts], lambda e: e.tensor_tensor(out=dts[0:M, 40:48], in0=dts[0:M, 32:40], in1=dts[0:M, 24:32], op=ALU.subtract))
                kb.op('act', [dts], [dts], lambda e: e.activation(out=dts[0:M, 40:48], in_=dts[0:M, 40:48], func=AF.Exp))
                kb.op('dve', [dts], [dts], lambda e: e.tensor_tensor(out=dts[0:M, 48:56], in0=dts[0:M, 40:48], in1=dts[0:M, 0:8], op=ALU.mult))
            if split == 1:
                return
            if split == 2:
                xs3 = xsB[0:M, 0:512].rearrange("p (h d) -> p h d", h=8)
                kb.op('dve', [xsB, dts], [xdte], lambda e: e.tensor_tensor(out=xdte[0:M], in0=xs3, in1=bc(dts[0:M, 48:56].unsqueeze(2), [M, 8, 64]), op=ALU.mult))
                return
            kb.op('dve', [dts], [dts], lambda e: e.tensor_tensor(out=dts[0:M, 40:48], in0=dts[0:M, 32:40], in1=dts[0:M, 24:32], op=ALU.subtract))
            kb.op('act', [dts], [dts], lambda e: e.activation(out=dts[0:M, 40:48], in_=dts[0:M, 40:48], func=AF.Exp))
            kb.op('dve', [dts], [dts], lambda e: e.tensor_tensor(out=dts[0:M, 48:56], in0=dts[0:M, 40:48], in1=dts[0:M, 0:8], op=ALU.mult))
            xs3 = xsB[0:M, 0:512].rearrange("p (h d) -> p h d", h=8)
            kb.op('dve', [xsB, dts], [xdte], lambda e: e.tensor_tensor(out=xdte[0:M], in0=xs3, in1=bc(dts[0:M, 48:56].unsqueeze(2), [M, 8, 64]), op=ALU.mult))

        def state_matmul_and_update(M, bank=None):
            B3 = banks[3] if bank is None else bank
            for g in range(2):
                kb.op('pe', [xsB, xdte], [B3], lambda e: e.matmul(B3.f[:, g * 256:(g + 1) * 256], lhsT=xsB[0:M, 512 + g * 128:512 + (g + 1) * 128], rhs=xdte[0:M, 4 * g:4 * g + 4, :], start=True, stop=True))
            kb.op('dve', [hT, dts], [F2], lambda e: e.tensor_tensor(out=tmpv, in0=hT[:], in1=bc(dts[:, 56:64].unsqueeze(2), [128, 8, 64]), op=ALU.mult))
            kb.op('dve', [F2, B3], [hT], lambda e: e.tensor_tensor(out=hT[:], in0=tmpv, in1=B3.f[:, :].rearrange("p (h d) -> p h d", h=8), op=ALU.add))

        kb.op('dve', [], [hT], lambda e: e.memset(hT[:], 0.0))
        kb.op('dve', [], [hTb], lambda e: e.memset(hTb[:], 0.0))
        kb.op('dve', [], [cbuf], lambda e: e.memset(cbuf[:], 0.0))

        def warm_slot(s):
            xsb = xs2[s % 2]
            kb.load('sp', xsb, xsb[:], xw[s * 128:(s + 1) * 128, :])
            norm_and_T(xsb, 128)
            last = (s == NB_WARM - 1)
            proj_xbc(128, range(8) if last else range(6))
            if last:
                proj_tok(B2, 256, 512, 128)
            proj_tok(B1, 8, 2304, 128)
            if last:
                kv_block(B2, 16, 128, 1, rowmask[:, s:s + 1].unsqueeze(2))
            conv_silu(B4, B5, 128, range(8) if last else range(6), 1, cbuf)
            conv_shift(128)
            xs_B_tokmajor(128)
            dt_compute(B1, 0, 128, rowmask[:, s:s + 1])
            ssd_state_update(128, trif[:, :], onesf[:, :], hT, hTb)
            state_matmul_and_update(128)

        build_dg()

        wc = [H.off]

        def walloc(name, shape, dt):
            b = alloc(name, shape, dt, at=wc[0]); wc[0] += b.words
            assert wc[0] <= regA, (name, wc[0], regA)
            return b
        wx = walloc("wx", [128, 4, D], F32)
        wxb = walloc("wxb", [128, 4, D], BF16)
        wxT = walloc("wxT", [128, 8, 512], BF16)
        wcb = walloc("wcb", [128, 6, 515], BF16)
        wsg = walloc("wsg", [128, 3, 512], F32)
        wG2 = walloc("wG2", [128, 6, 512], BF16)
        wxsB = walloc("wxsB", [128, 4, 768], BF16)
        wdts = walloc("wdts", [128, 8, 32], F32)
        wxdte = walloc("wxdte", [128, 4, 512], BF16)
        wst = walloc("wst", [128, 16], F32)
        kb.op('dve', [], [wcb], lambda e: e.memset(wcb[:], 0.0))

        def warm_group(g):
            s0 = 4 * g
            kb.load('sp', wx, wx[:], xw[s0 * 128:(s0 + 4) * 128, :].rearrange("(s p) d -> p s d", p=128))
            for s in range(4):
                kb.op('act', [wx], [wxb, wst], lambda e: e.activation(out=wxb[:, s, :], in_=wx[:, s, :], func=AF.Square, accum_out=wst[:, s:s + 1]))
            kb.op('act', [wst, prm], [wst], lambda e: e.activation(out=wst[:, 4:8], in_=wst[:, 0:4], func=AF.Ln, scale=1.0 / D, bias=epsb))
            kb.op('act', [wst], [wst], lambda e: e.activation(out=wst[:, 4:8], in_=wst[:, 4:8], func=AF.Exp, scale=-0.5))
            for s in range(4):
                kb.op('dve', [wx, wst], [wxb], lambda e: e.tensor_scalar(out=wxb[:, s, :], in0=wx[:, s, :], scalar1=wst[:, 4 + s:5 + s], scalar2=None, op0=ALU.mult))
            for s in range(4):
                tbk = T0 if s % 2 == 0 else B7
                for kt in range(8):
                    kb.op('pe', [wxb, ident], [tbk], lambda e: e.transpose(out=tbk.b[:, kt * 128:(kt + 1) * 128], in_=wxb[:, s, kt * 128:(kt + 1) * 128], identity=ident[:, :]), fin=(kt == 7))
                if s % 2 == 0:
                    kb.op('act', [tbk], [wxT], lambda e: e.copy(out=wxT[:, :, s * 128:(s + 1) * 128], in_=tbk.b[:, :].rearrange("p (a b) -> p a b", a=8)))
                else:
                    kb.op('dve', [tbk], [wxT], lambda e: e.tensor_copy(out=wxT[:, :, s * 128:(s + 1) * 128], in_=tbk.b[:, :].rearrange("p (a b) -> p a b", a=8)))
            for s in range(4):
                for kt in range(8):
                    kb.op('pe', [wxT, Win], [B6], lambda e: e.matmul(B6.f[:, s * 8:(s + 1) * 8], lhsT=wxT[:, kt, s * 128:(s + 1) * 128], rhs=Win[:, kt, 2304:2312], start=(kt == 0), stop=(kt == 7)), fin=(kt == 7))
            bk4 = [B1, B2, B3, B4]
            for ct in range(6):
                bank = bk4[ct % 4]
                for kt in range(8):
                    kb.op('pe', [wxT, Win], [bank], lambda e: e.matmul(bank.f[:, :], lhsT=Win[:, kt, 1280 + ct * 128:1280 + (ct + 1) * 128], rhs=wxT[:, kt, :], start=(kt == 0), stop=(kt == 7)), fin=(kt == 7))
                if ct % 2 == 0:
                    kb.op('act', [bank], [wcb], lambda e: e.copy(out=wcb[:, ct, 3:515], in_=bank.f[:, :]))
                else:
                    kb.op('dve', [bank], [wcb], lambda e: e.tensor_copy(out=wcb[:, ct, 3:515], in_=bank.f[:, :]))
            v4 = lambda r: wdts[:, r, :].rearrange("p (s h) -> p s h", s=4)
            kb.op('dve', [B6, prm], [wdts], lambda e: e.tensor_tensor(out=v4(0), in0=B6.f[:, 0:32].rearrange("p (s h) -> p s h", s=4), in1=bc(prm[:, 0:8].unsqueeze(1), [128, 4, 8]), op=ALU.add))
            kb.op('act', [wdts], [wdts], lambda e: e.activation(out=wdts[:, 0, :], in_=wdts[:, 0, :], func=AF.Exp))
            kb.op('act', [wdts, prm], [wdts], lambda e: e.activation(out=wdts[:, 0, :], in_=wdts[:, 0, :], func=AF.Ln, bias=prm[:, 49:50]))
            kb.op('dve', [wdts, rowmask], [wdts], lambda e: e.tensor_tensor(out=v4(0), in0=v4(0), in1=bc(rowmask[:, s0:s0 + 4].unsqueeze(2), [128, 4, 8]), op=ALU.mult))
            kb.op('dve', [wdts, prm], [wdts], lambda e: e.tensor_tensor(out=v4(1), in0=v4(0), in1=bc(prm[:, 32:40].unsqueeze(1), [128, 4, 8]), op=ALU.mult))
            kb.op('pe', [wdts], [B5], lambda e: e.matmul(B5.f[:, 0:32], lhsT=trif[:, :], rhs=wdts[:, 1, :], start=True, stop=True))
            kb.op('pe', [wdts], [B5], lambda e: e.matmul(B5.f[:, 32:64], lhsT=onesf[:, :], rhs=wdts[:, 1, :], start=True, stop=True))
            kb.op('act', [B5], [wdts], lambda e: e.copy(out=wdts[:, 2:4, :].rearrange("p a b -> p (a b)"), in_=B5.f[:, 0:64]))
            kb.op('act', [B5], [wdts], lambda e: e.activation(out=wdts[:, 6, :], in_=B5.f[:, 32:64], func=AF.Exp))
            kb.op('dve', [wdts], [wdts], lambda e: e.tensor_tensor(out=wdts[:, 4, :], in0=wdts[:, 3, :], in1=wdts[:, 2, :], op=ALU.subtract))
            kb.op('act', [wdts], [wdts], lambda e: e.activation(out=wdts[:, 4, :], in_=wdts[:, 4, :], func=AF.Exp))
            kb.op('dve', [wdts], [wdts], lambda e: e.tensor_tensor(out=wdts[:, 5, :], in0=wdts[:, 4, :], in1=wdts[:, 0, :], op=ALU.mult))
            bk3 = [B1, B2, B3]
            for half in range(2):
                for i in range(3):
                    ct = 3 * half + i
                    bank = bk3[i]
                    for j in range(4):
                        kb.op('pe', [Dg, wcb], [bank], lambda e: e.matmul(bank.f[:, :], lhsT=Dg[:, ct, j, :], rhs=wcb[:, ct, j:j + 512], start=(j == 0), stop=False), fin=False)
                    kb.op('pe', [cbrow, onesrow], [bank], lambda e: e.matmul(bank.f[:, :], lhsT=cbrow[0:1, ct * 128:(ct + 1) * 128], rhs=onesrow[0:1, 0:512], start=False, stop=True))
                    kb.op('act', [bank], [wG2], lambda e: e.activation(out=wG2[:, ct, :], in_=bank.f[:, :], func=AF.Silu))
            kb.op('pool', [wcb], [wcb], lambda e: e.tensor_copy(out=wcb[:, :, 0:3], in_=wcb[:, :, 512:515]))
            for s in range(4):
                tbk = T0 if s % 2 == 0 else B7
                for ct in range(6):
                    kb.op('pe', [wG2, ident], [tbk], lambda e: e.transpose(out=tbk.b[:, ct * 128:(ct + 1) * 128], in_=wG2[:, ct, s * 128:(s + 1) * 128], identity=ident[:, :]), fin=(ct == 5))
                if s % 2 == 0:
                    kb.op('act', [tbk], [wxsB], lambda e: e.copy(out=wxsB[:, s, :], in_=tbk.b[:, 0:768]))
                else:
                    kb.op('dve', [tbk], [wxsB], lambda e: e.tensor_copy(out=wxsB[:, s, :], in_=tbk.b[:, 0:768]))
            kb.op('dve', [wxsB, wdts], [wxdte], lambda e: e.tensor_tensor(out=wxdte[:].rearrange("p s (h d) -> p s h d", h=8), in0=wxsB[:, :, 0:512].rearrange("p s (h d) -> p s h d", h=8), in1=bc(v4(5).unsqueeze(3), [128, 4, 8, 64]), op=ALU.mult))
            for s in range(4):
                bank = bk4[s]
                for gg in range(2):
                    kb.op('pe', [wxsB, wxdte], [bank], lambda e: e.matmul(bank.f[:, gg * 256:(gg + 1) * 256], lhsT=wxsB[:, s, 512 + gg * 128:512 + (gg + 1) * 128], rhs=wxdte[:, s, gg * 256:(gg + 1) * 256], start=True, stop=True))
                kb.op('dve', [hT, wdts], [F2], lambda e: e.tensor_tensor(out=tmpv, in0=hT[:], in1=bc(wdts[:, 6, s * 8:(s + 1) * 8].unsqueeze(2), [128, 8, 64]), op=ALU.mult))
                kb.op('dve', [F2, bank], [hT], lambda e: e.tensor_tensor(out=hT[:], in0=tmpv, in1=bank.f[:, :].rearrange("p (h d) -> p h d", h=8), op=ALU.add))

        for g in range(4):
            warm_group(g)
        kb.op('pool', [wcb], [cbuf], lambda e: e.tensor_copy(out=cbuf[:, 0:6, 0:3], in_=wcb[:, :, 0:3]))
        warm_slot(NB_WARM - 1)
        kb.op('act', [hT], [hTb], lambda e: e.copy(out=hTb[:], in_=hT[:]))

        def attn_finish(M):
            g6v = G6[0:M, 0:512].rearrange("p (h d) -> p h d", h=8)
            for kvh in range(2):
                bank = B1 if kvh == 0 else B2
                o3 = bank.f[0:M, 0:260].rearrange("p (h d) -> p h d", h=4)
                kb.op('dve', [bank, prm], [st], lambda e: e.tensor_tensor(out=st[0:M, 8 + 4 * kvh:12 + 4 * kvh].unsqueeze(2), in0=o3[:, :, 64:65], in1=prm[0:M, 40 + 4 * kvh:44 + 4 * kvh].unsqueeze(2), op=ALU.add))
                kb.op('dve', [st], [st], lambda e: e.reciprocal(out=st[0:M, 8 + 4 * kvh:12 + 4 * kvh], in_=st[0:M, 8 + 4 * kvh:12 + 4 * kvh]))
                kb.op('dve', [bank, st], [G6], lambda e: e.tensor_tensor(out=g6v[:, 4 * kvh:4 * kvh + 4, :], in0=o3[:, :, 0:64], in1=bc(st[0:M, 8 + 4 * kvh:12 + 4 * kvh].unsqueeze(2), [M, 4, 64]), op=ALU.mult))
            kb.op('act', [G6], [qtm, st], lambda e: e.activation(out=qtm[0:M].rearrange("p h d -> p (h d)"), in_=G6[0:M, 0:512], func=AF.Square, accum_out=st[0:M, 2:3]))
            rstd_from_sumsq(2, 3, M, 512)
            kb.op('dve', [G6, st], [G6], lambda e: e.tensor_scalar(out=G6[0:M, 0:512], in0=G6[0:M, 0:512], scalar1=st[0:M, 3:4], scalar2=None, op0=ALU.mult))

        def ssd_intra_g(M, cummat_bcast_rhs, gmask, split=False):
            for h in range(8):
                bank = B4 if h < 4 else B5
                o = (h % 4) * 128
                kb.op('pe', [dts], [bank], lambda e: e.matmul(bank.f[0:M, o:o + M], lhsT=bc(dts[0:M, 8 + h:9 + h], [M, M]), rhs=cummat_bcast_rhs, start=True, stop=True))
            for g in range(2):
                kb.op('pe', [gsel['g2']], [B3], lambda e: e.matmul(B3.f[0:M, g * 128:g * 128 + M], lhsT=gsel['g2'][:, 4 + g, 0:M], rhs=gsel['g2'][:, 6 + g, 0:M], start=True, stop=True))
            kb.op('dve', [B3], [GTm], lambda e: e.tensor_tensor(out=GTm[0:M, :, 0:M], in0=B3.f[0:M, 0:256].rearrange("p (a b) -> p a b", a=2)[:, :, 0:M], in1=bc(gmask.unsqueeze(1), [M, 2, M]), op=ALU.mult))
            if split:
                yield
            for h in range(8):
                bank = B4 if h < 4 else B5
                o = (h % 4) * 128
                kb.op('dve', [bank, dts], [F1], lambda e: e.tensor_scalar(out=F1[0:M, h, 0:M], in0=bank.f[0:M, o:o + M], scalar1=dts[0:M, 24 + h:25 + h], scalar2=0.0, op0=ALU.subtract, op1=ALU.min))
            kb.op('act', [F1], [G4], lambda e: e.activation(out=G4[0:M, :, 0:M], in_=F1[0:M, :, 0:M], func=AF.Exp))
            for g in range(2):
                kb.op('dve', [G4, GTm], [G4], lambda e: e.tensor_tensor(out=G4[0:M, 4 * g:4 * g + 4, 0:M], in0=G4[0:M, 4 * g:4 * g + 4, 0:M], in1=bc(GTm[0:M, g:g + 1, 0:M], [M, 4, M]), op=ALU.mult))

        def ssd_intra(M, cummat_bcast_rhs, gmask):
            for _ in ssd_intra_g(M, cummat_bcast_rhs, gmask, split=False):
                pass

        def front_F(j, xsrc):
            xsb = xs2[(j + NB_WARM) % 2]
            kb.load('sp', xsb, xsb[:], xsrc)
            norm_and_T(xsb, 128)

        def front_P_qkv(j):
            M = 128
            proj_tok(B1, 512, 0, M)
            proj_tok(B2, 256, 512, M)
            proj_tok(B2, 8, 2304, M, co=272)

        def front_P_xbc(j):
            M = 128
            proj_xbc(M, range(8), bks=(B6, B7))
            yield
            conv_silu(B6, B7, M, range(8), 1, cbuf, g2=G2s[j % 2])
            yield

        def front_P_z(j):
            M = 128
            proj_tok(B3, 512, 768, M)
            kb.op('act', [B3], [F2], lambda e: e.activation(out=F2[0:M, 0:4, :].rearrange("p a b -> p (a b)"), in_=B3.f[0:M, :], func=AF.Silu))

        def chain_Q(j, slot, last_of_core):
            M = 128
            par = j % 2
            rope_apply(B1, 0, 8, slot, M, None, None, qtm[0:M], qtm)
            yield
            for h in range(8):
                kb.op('pe', [qtm, ident], [T0], lambda e: e.transpose(out=T0.b[0:64, h * 128:h * 128 + M], in_=qtm[0:M, h, :], identity=ident[0:M, 0:M]), fin=(h == 7))
            kb.op('act', [T0], [qT], lambda e: e.copy(out=qT[0:64, :, 0:M], in_=T0.b[0:64, :].rearrange("p (a b) -> p a b", a=8)[:, :, 0:M]))
            yield
            kv_block(B2, slot, M, par, prm[:, 49:50].unsqueeze(2))
            if last_of_core:
                kb.store('sp', kf, kp_o, kf[:].rearrange("p h d -> p (h d)"))
                kb.store('sp', vf, vp_o, vf[:].rearrange("p h d -> p (h d)"))
            yield

        def stage_back_gen(j, blk_in_sg):
            xsb = xs2[(j + NB_WARM) % 2]
            return out_proj_gen(128, xsb, Hb[blk_in_sg][0:128, :], Hb[blk_in_sg], HnT[:, :, blk_in_sg * 128:blk_in_sg * 128 + 128], HnT, banks=(B4, B5), mixbuf=G7b)

        def run_merged(*gens):
            gens = [g for g in gens if g is not None]
            while gens:
                alive = []
                for g in gens:
                    try:
                        next(g)
                        alive.append(g)
                    except StopIteration:
                        pass
                gens = alive

        def stage_mid(j, blk_in_sg, slot, last_of_core, hook, backgen=None, q_done=False, early=None, prest=None):
            M = 128
            par = j % 2
            gsel['g2'] = G2s[j % 2]
            zs = F2[0:M, 0:4, :].rearrange("p a b -> p (a b)")

            def gen_D():
                dt_compute(B2, 272, M, None)
                yield
                ssd_state_update(M, trif[:, :], onesf[:, :], hT, hTb, bank=B2, split=1, sc=280)
                yield

            def gen_Q():
                if q_done:
                    return
                yield from chain_Q(j, slot, last_of_core)

            def gen_C():
                conv_shift(M)
                yield

            def gen_B1():
                if backgen is not None:
                    next(backgen)
                yield

            def gen_F():
                if hook is not None:
                    hook()
                yield

            if last_of_core:
                for _ in gen_D():
                    pass
                for half in range(2):
                    proj_tok(B6 if half == 0 else B7, 512, 1280 + half * 512, M)
                kb.op('act', [B6], [F1], lambda e: e.copy(out=xtv[0:M, 0:512], in_=B6.f[0:M, :]))
                kb.op('act', [B7], [F1], lambda e: e.copy(out=xtv[0:M, 512:1024], in_=B7.f[0:M, :]))
                kb.store('sp', F1, cp_o, xtv[125:128, :])
                run_merged(gen_B1(), gen_Q(), gen_C(), gen_F())
            else:
                run_merged(gen_B1(), gen_D(), gen_Q(), gen_C(), gen_F())

            def gen_A():
                sb = [B6, B7]
                i = 0
                for kvh in range(2):
                    for kbk, pp in ((0, 1 - par), (1, par)):
                        bank = sb[i % 2]; i += 1
                        kb.op('pe', [kT[pp], qT], [bank], lambda e: e.matmul(bank.f[:, 0:4 * M], lhsT=kT[pp][0:64, kvh, :], rhs=qT[0:64, 4 * kvh:4 * kvh + 4, 0:M], start=True, stop=True))
                        PTs = PTb[kbk * 2 + kvh]
                        kb.op('act', [bank], [PTs], lambda e: e.activation(out=PT[:, kbk, 4 * kvh:4 * kvh + 4, 0:M], in_=bank.f[:, 0:4 * M].rearrange("p (a b) -> p a b", a=4), func=AF.Exp, scale=0.125))
                        mk = maskp if kbk == 0 else maskd
                        kb.op('dve', [PTs, mk], [PTs], lambda e: e.tensor_tensor(out=PT[:, kbk, 4 * kvh:4 * kvh + 4, 0:M], in0=PT[:, kbk, 4 * kvh:4 * kvh + 4, 0:M], in1=bc(mk[:, 0:M].unsqueeze(1), [128, 4, M]), op=ALU.mult))
                    yield
                for kvh in range(2):
                    bank = B1 if kvh == 0 else B2
                    for r in range(4):
                        h = 4 * kvh + r
                        for kbk, pp in ((0, 1 - par), (1, par)):
                            kb.op('pe', [PTb[kbk * 2 + kvh], Va[pp]], [bank], lambda e: e.matmul(bank.f[0:M, r * 65:(r + 1) * 65], lhsT=PT[:, kbk, h, 0:M], rhs=Va[pp][:, kvh, :], start=(kbk == 0), stop=(kbk == 1)), fin=(kbk == 1 and r == 3))
                attn_finish(M)
                yield
                if early is not None:
                    yield from early()

            def gen_S():
                xs_B_tokmajor(M)
                yield
                ssd_state_update(M, trif[:, :], onesf[:, :], hT, hTb, split=2)
                xs3 = xsB[0:M, 0:512].rearrange("p (h d) -> p h d", h=8)
                kb.op('dve', [xsB, dts], [xdt], lambda e: e.tensor_tensor(out=xdt[0:M], in0=xs3, in1=bc(dts[0:M, 0:8].unsqueeze(2), [M, 8, 64]), op=ALU.mult))
                kb.op('pool', [xsB, prm], [xskip], lambda e: e.tensor_tensor(out=xskip[0:M], in0=xs3, in1=bc(prm[0:M, 16:24].unsqueeze(2), [M, 8, 64]), op=ALU.mult))
                yield
                yield from ssd_intra_g(M, trif[:, :], maskd[:, :], split=True)
                yield
                for hh in range(2):
                    bank = B4 if hh == 0 else B5
                    kb.op('act', [bank], [G5], lambda e: e.activation(out=G5[:, 4 * hh:4 * hh + 4, :], in_=bank.f[:, :].rearrange("p (a b) -> p a b", a=4), func=AF.Exp))
                for g in range(2):
                    kb.op('dve', [G5, gsel['g2']], [G5], lambda e: e.tensor_tensor(out=G5[:, 4 * g:4 * g + 4, :], in0=G5[:, 4 * g:4 * g + 4, :], in1=bc(gsel['g2'][:, 6 + g:7 + g, :], [128, 4, 128]), op=ALU.mult))
                yield
                for h in range(8):
                    kb.op('pe', [G4, xdt], [B3], lambda e: e.matmul(B3.f[0:M, h * 64:(h + 1) * 64], lhsT=G4[0:M, h, 0:M], rhs=xdt[0:M, h, :], start=True, stop=False), fin=False)
                    kb.op('pe', [G5, hTb], [B3], lambda e: e.matmul(B3.f[0:M, h * 64:(h + 1) * 64], lhsT=G5[:, h, 0:M], rhs=hTb[:, h, :], start=False, stop=True), fin=(h == 7))
                yield
                state_matmul_and_update(M, bank=B4)
                kb.op('act', [hT], [hTb], lambda e: e.copy(out=hTb[:], in_=hT[:]))
                yield
                if last_of_core:
                    for jt in range(4):
                        kb.op('pe', [hT, identf], [B5], lambda e: e.transpose(out=B5.f[:, jt * 128:(jt + 1) * 128], in_=hT[:].rearrange("p h d -> p (h d)")[:, jt * 128:(jt + 1) * 128], identity=identf[:, :]))
                    kb.op('act', [B5], [F1], lambda e: e.copy(out=xtv[:, 0:512], in_=B5.f[:, :]))
                    kb.store('sp', F1, sp_o.rearrange("(j p) n -> p j n", p=128), xtv[:, 0:512].rearrange("p (j n) -> p j n", j=4))
                ysf = tmpv[0:M].rearrange("p h d -> p (h d)")
                kb.op('dve', [B3, xskip], [F2], lambda e: e.tensor_tensor(out=ysf, in0=B3.f[0:M, :], in1=xskip[0:M].rearrange("p h d -> p (h d)"), op=ALU.add))
                if prest is not None:
                    prest()
                kb.op('dve', [F2, F2], [F2], lambda e: e.tensor_tensor(out=ysf, in0=ysf, in1=zs, op=ALU.mult))
                kb.op('act', [F2], [G6, st], lambda e: e.activation(out=G6[0:M, 512:1024], in_=ysf, func=AF.Square, accum_out=st[0:M, 4:5]))
                rstd_from_sumsq(4, 5, M, 512)
                kb.op('dve', [F2, st], [G6], lambda e: e.tensor_scalar(out=G6[0:M, 512:1024], in0=ysf, scalar1=st[0:M, 5:6], scalar2=None, op0=ALU.mult))
                yield

            run_merged(gen_A(), gen_S(), backgen)

        def out_proj_gen(M, xsb, hdst, hbuf, hnT_dst, hnTbuf, banks=None, mixbuf=None):
            banks = (B1, B2) if banks is None else banks
            G7_ = G7 if mixbuf is None else mixbuf
            transpose_to(G7_, G7_[:, :, 0:M], G6, G6[0:M, :], M, 8)
            for half in range(2):
                bank = banks[half]
                for kt in range(8):
                    kb.op('pe', [G7_, Wout], [bank], lambda e: e.matmul(bank.f[0:M, :], lhsT=G7_[:, kt, 0:M], rhs=Wout[:, kt, half * 512:(half + 1) * 512], start=(kt == 0), stop=(kt == 7)), fin=(kt == 7))
            yield
            for half in range(2):
                bank = banks[half]
                kb.op('dve', [bank, xsb], [hbuf], lambda e: e.tensor_tensor(out=hdst[:, half * 512:(half + 1) * 512], in0=bank.f[0:M, :], in1=xsb[0:M, half * 512:(half + 1) * 512], op=ALU.add))
            yield
            kb.op('act', [hbuf], [G1, st], lambda e: e.activation(out=G1[0:M, :], in_=hdst, func=AF.Square, accum_out=st[0:M, 6:7]))
            rstd_from_sumsq(6, 7, M, D)
            kb.op('dve', [hbuf, st], [G1], lambda e: e.tensor_scalar(out=G1[0:M, :], in0=hdst, scalar1=st[0:M, 7:8], scalar2=None, op0=ALU.mult))
            yield
            yield
            transpose_to(hnTbuf, hnT_dst, G1, G1[0:M, :], M, 8)
            yield

        def out_proj_and_store(*a, **k):
            for _ in out_proj_gen(*a, **k):
                pass

        qstate = {'n': 0}

        def mlp_group(wq, src_ap, ntok, nblk, hviews, hbuf, ui, do_down=True):
            u = uT[ui % 2]
            wu, wd = wq
            for ft in range(8):
                bank = [B1, B2, B3, B4][ft % 4]
                for kt in range(8):
                    kb.op('pe', [wu, hnT_any], [bank], lambda e: e.matmul(bank.f[:, 0:ntok], lhsT=wu[:, kt, ft * 128:(ft + 1) * 128], rhs=src_ap[:, kt, :], start=(kt == 0), stop=(kt == 7)), fin=(kt == 7))
                r = ur[ft % 2]
                kb.op('act', [bank], [r], lambda e: e.activation(out=r[:, 0:ntok], in_=bank.f[:, 0:ntok], func=AF.Relu))
                kb.op('act', [r], [u], lambda e: e.activation(out=u[:, ft, 0:ntok], in_=r[:, 0:ntok], func=AF.Square))
            if not do_down:
                return
            mlp_down(wq, hviews, ui)

        def mlp_down(wq, hviews, ui):
            u = uT[ui % 2]
            wu, wd = wq
            k = 0
            for bi, (M, hv, hbuf) in enumerate(hviews):
                for half in range(2):
                    bank = [B5, B6, B7, T0][k % 4]; k += 1
                    for ft in range(8):
                        kb.op('pe', [u, wd], [bank], lambda e: e.matmul(bank.f[0:M, :], lhsT=u[:, ft, bi * 128:bi * 128 + M], rhs=wd[:, ft, half * 512:(half + 1) * 512], start=(ft == 0), stop=(ft == 7)), fin=(ft == 7))
                    kb.op('dve', [bank, hbuf], [hbuf], lambda e: e.tensor_tensor(out=hv[:, half * 512:(half + 1) * 512], in0=hv[:, half * 512:(half + 1) * 512], in1=bank.f[0:M, :], op=ALU.add))

        def final_norm_store(M, hv, hbuf, dst, oi):
            assert ur[1].off == ur[0].off + ur[0].words
            junk = arena[:, ur[0].off:ur[0].off + 512].bitcast(BF16)[0:M, :]
            kb.op('act', [hbuf], [ur[0], ur[1], stB], lambda e: e.activation(out=junk, in_=hv, func=AF.Square, accum_out=stB[0:M, 0:1]))
            kb.op('act', [stB, prm], [stB], lambda e: e.activation(out=stB[0:M, 1:2], in_=stB[0:M, 0:1], func=AF.Ln, scale=1.0 / D, bias=epsb[0:M, :]))
            kb.op('act', [stB], [stB], lambda e: e.activation(out=stB[0:M, 1:2], in_=stB[0:M, 1:2], func=AF.Exp, scale=-0.5))
            kb.op('dve', [hbuf, stB, gfb], [hbuf], lambda e: e.scalar_tensor_tensor(out=hv, in0=hv, scalar=stB[0:M, 1:2], in1=gfb[0:M, :], op0=ALU.mult, op1=ALU.mult))
            kb.store('sp', hbuf, dst, hv)

        hnT_any = HnT

        def phase_b(sg, include_sample):
            kb.barrier()
            kb.load('sp', gfb, gfb[:], pbc(norm_f_g, D))
            items = []
            for c in range(4):
                for tg in range(SG // 4):
                    hv = [(128, Hb[tg * 4 + b][:, :], Hb[tg * 4 + b]) for b in range(4)]
                    items.append((c, HnT[:, :, tg * 512:(tg + 1) * 512], 512, hv, tg))
                if include_sample:
                    items.append((c, HnTs[:, :, :], 64, [(64, Hs[0:64, :], Hs)], -1))

            def finish_item(i):
                c, src, ntok, hv, tg = items[i]
                mlp_down(Wq[c % 2], hv, i)
                last_of_quarter = (i + 1 == len(items)) or (items[i + 1][0] != c)
                if last_of_quarter and c + 2 < 4:
                    load_quarter(Wq[c % 2], c + 2)
                if c == 3 and tg >= 0:
                    for b in range(4):
                        bb = tg * 4 + b
                        final_norm_store(128, Hb[bb][:, :], Hb[bb], y_f[(sg * SG + bb) * 128:(sg * SG + bb + 1) * 128, :], bb)

            load_quarter(Wq[1], 1)
            for i, (c, src, ntok, hv, tg) in enumerate(items):
                mlp_group(Wq[c % 2], src, ntok, len(hv), hv, None, i, do_down=False)
                if i > 0:
                    finish_item(i - 1)
            finish_item(len(items) - 1)
            if include_sample:
                final_norm_store(64, Hs[0:64, :], Hs, y_s, SG)
            kb.barrier()

        def sample_phase():
            M = 64
            SL = 17
            base = H.off
            c2 = [base]

            def salloc(name, shape, dt):
                b = alloc(name, shape, dt, at=c2[0]); c2[0] += b.words
                assert c2[0] <= regA, (name, c2[0], regA)
                return b
            S0s = [salloc("S0a", [128, 4, 4, 128], F32), salloc("S0b", [128, 4, 4, 128], F32)]
            h0Ts = [salloc("h0T%d" % i, [128, 512], BF16) for i in range(4)]
            S0h = salloc("S0h", [128, 4, 4, 128], BF16)
            ckb = salloc("ckb", [128, 16, 128], BF16)
            kTc = salloc("kTc", [128, 16, 2, 128], BF16)
            Vc = salloc("Vc", [128, 16, 2, 65], BF16)
            PcP = salloc("PcP", [128, 8, 16, 64], BF16)
            Pn = salloc("Pn", [128, 8, 64], BF16)
            CTp = salloc("CTp", [128, 2, 16, 64], BF16)
            Bmk = salloc("Bmk", [128, 16, 2, 128], BF16)
            cbs = salloc("cbs", [128, 8, 112], BF16)
            smask = salloc("smask", [128, 256], F32)
            kb.load('sp', smask, smask[:], smask_d)
            sctm = salloc("sctm", [128, D], F32)
            totT = salloc("totT", [128, 4, 16], F32)
            eac = salloc("eac", [128, 8], F32)
            yos = salloc("yos", [128, 8, 64], F32)
            kfs = salloc("kfs", [128, 2, 64], F32)
            bdf = smask[0:64, 0:64]
            seqm = smask[0:64, 64:80]
            cmask = smask[:, 80:84]
            bdb = salloc("bdb", [128, 64], BF16)
            cmb = salloc("cmb", [128, 4], BF16)
            kb.op('dve', [smask], [bdb], lambda e: e.tensor_copy(out=bdb[0:64, :], in_=bdf))
            kb.op('dve', [smask], [cmb], lambda e: e.tensor_copy(out=cmb[:, :], in_=cmask))

            xsb = xs2[0]
            kb.load('sp', xsb, xsb[0:M, :], xsm)
            kb.load('pool', ckb, ckb[:], cache_k.rearrange("b s c -> s b c"))
            kb.op('dve', [], [Vc], lambda e: e.memset(Vc[:], 1.0))
            for kvh in range(2):
                kb.load('pool', Vc, Vc[:, :, kvh, 0:64], cache_v[:, :, kvh * 64:(kvh + 1) * 64].rearrange("b s d -> s b d"))
            dd = Buf("dd", None)
            kb.load('sp', dd, ks_o[:, 0:124, :], cache_k[:, 4:128, :])
            kb.load('sp', dd, vs_o[:, 0:124, :], cache_v[:, 4:128, :])
            for jj in range(3):
                kb.load('sp', sctm, sctm[jj * 16:(jj + 1) * 16, :], state_conv[:, jj, :])
            for ct in range(8):
                bank = B4 if ct < 4 else B5
                o = (ct % 4) * 128
                kb.op('pe', [sctm, identf], [bank], lambda e: e.transpose(out=bank.f[:, o:o + 48], in_=sctm[0:48, ct * 128:(ct + 1) * 128], identity=identf[0:48, 0:48]))
            kb.op('act', [B4], [cbs], lambda e: e.copy(out=cbs[:, 0:4, 0:48], in_=B4.f[:, :].rearrange("p (a b) -> p a b", a=4)[:, :, 0:48]))
            kb.op('act', [B5], [cbs], lambda e: e.copy(out=cbs[:, 4:8, 0:48], in_=B5.f[:, :].rearrange("p (a b) -> p a b", a=4)[:, :, 0:48]))

            norm_and_T(xsb, M)
            proj_tok(B1, 512, 0, M)
            proj_tok(B2, 256, 512, M)
            proj_tok(B3, 512, 768, M)
            proj_xbc(M, range(8))
            rope_apply(B1, 0, 8, SL, M, None, None, qtm[0:M], qtm)
            for h in range(8):
                kb.op('pe', [qtm, ident], [T0], lambda e: e.transpose(out=T0.b[0:64, h * 128:h * 128 + M], in_=qtm[0:M, h, :], identity=ident[0:M, 0:M]))
            kb.op('act', [T0], [qT], lambda e: e.copy(out=qT[0:64, :, 0:M], in_=T0.b[0:64, :].rearrange("p (a b) -> p a b", a=8)[:, :, 0:M]))
            kv_block(B2, SL, M, 0, prm[:, 49:50].unsqueeze(2))
            for t in range(4):
                kb.store('sp', kf, ks_o[:, 124 + t, :], kf[16 * t:16 * t + 16].rearrange("p h d -> p (h d)"))
                kb.store('sp', vf, vs_o[:, 124 + t, :], vf[16 * t:16 * t + 16].rearrange("p h d -> p (h d)"))
            conv_silu(B4, B5, M, range(8), 16, cbs)
            for half in range(2):
                proj_tok(B6 if half == 0 else B7, 512, 1280 + half * 512, M)
            kb.op('act', [B6], [F1], lambda e: e.copy(out=xtv[0:M, 0:512], in_=B6.f[0:M, :]))
            kb.op('act', [B7], [F1], lambda e: e.copy(out=xtv[0:M, 512:1024], in_=B7.f[0:M, :]))
            for jj in range(3):
                kb.store('sp', F1, cs_o[:, jj, :], xtv[16 * (jj + 1):16 * (jj + 2), :])
            zs = F2[0:M, 0:4, :].rearrange("p a b -> p (a b)")
            kb.op('act', [B3], [F2], lambda e: e.activation(out=zs, in_=B3.f[0:M, :], func=AF.Silu))

            for b in range(16):
                for kvh in range(2):
                    kb.op('pe', [ckb, ident], [T0], lambda e: e.transpose(out=T0.b[0:64, ((b % 4) * 2 + kvh) * 128:((b % 4) * 2 + kvh + 1) * 128], in_=ckb[:, b, kvh * 64:(kvh + 1) * 64], identity=ident[:, :]))
                if b % 4 == 3:
                    b0 = b - 3
                    kb.op('act', [T0], [kTc], lambda e: e.copy(out=kTc[0:64, b0:b0 + 4, :, :], in_=T0.b[0:64, :].rearrange("p (a c d) -> p a c d", a=4, c=2)))
            for b in range(16):
                for kvh in range(2):
                    rhs = qT[0:64, 4 * kvh:4 * kvh + 4, b:64:16]
                    kb.op('pe', [kTc, qT], [B6], lambda e: e.matmul(B6.f[:, b * 32 + kvh * 16:b * 32 + kvh * 16 + 16], lhsT=kTc[0:64, b, kvh, :], rhs=rhs, start=True, stop=True))
            kb.op('dve', [], [PcP], lambda e: e.memset(PcP[:], 0.0))
            kb.op('act', [B6], [G4], lambda e: e.activation(out=G4[:, 0:4, :].rearrange("p a b -> p (a b)"), in_=B6.f[:, :], func=AF.Exp, scale=0.125))
            src = G4[:, 0:4, :].rearrange("p a b -> p (a b)").rearrange("p (b h t) -> p b h t", b=16, h=8)
            pc_t = PcP.t
            dst = bass.AP(pc_t.tensor, pc_t.offset, [list(pc_t.ap[0]), [65, 16], [1024, 8], [16, 4]])
            cm_t = cmb.t
            cmv = bass.AP(cm_t.tensor, cm_t.offset, [list(cm_t.ap[0]), [0, 16], [0, 8], [1, 4]])
            kb.op('dve', [G4, cmb], [PcP], lambda e: e.tensor_tensor(out=dst, in0=src, in1=cmv, op=ALU.mult))
            for kvh in range(2):
                kb.op('pe', [kT[0], qT], [B7], lambda e: e.matmul(B7.f[0:M, kvh * 256:(kvh + 1) * 256], lhsT=kT[0][0:64, kvh, 0:M], rhs=qT[0:64, 4 * kvh:4 * kvh + 4, 0:M], start=True, stop=True))
            kb.op('act', [B7], [Pn], lambda e: e.activation(out=Pn[0:M, :, :], in_=B7.f[0:M, :].rearrange("p (a b) -> p a b", a=8), func=AF.Exp, scale=0.125))
            kb.op('dve', [Pn, bdb], [Pn], lambda e: e.tensor_tensor(out=Pn[0:M, :, :], in0=Pn[0:M, :, :], in1=bc(bdb[0:M, :].unsqueeze(1), [M, 8, 64]), op=ALU.mult))
            for kvh in range(2):
                bank = B1 if kvh == 0 else B2
                for r in range(4):
                    h = 4 * kvh + r
                    for b in range(16):
                        kb.op('pe', [PcP, Vc], [bank], lambda e: e.matmul(bank.f[0:M, r * 65:(r + 1) * 65], lhsT=PcP[:, h, b, :], rhs=Vc[:, b, kvh, :], start=(b == 0), stop=False))
                    kb.op('pe', [Pn, Va[0]], [bank], lambda e: e.matmul(bank.f[0:M, r * 65:(r + 1) * 65], lhsT=Pn[0:M, h, :], rhs=Va[0][0:M, kvh, :], start=False, stop=True))
            attn_finish(M)

            xs_B_tokmajor(M)
            proj_tok(B6, 8, 2304, M)
            dt_compute(B6, 0, M, None)
            kb.op('pe', [dts, smask], [B2], lambda e: e.matmul(B2.f[0:M, 256:264], lhsT=bdf, rhs=dts[0:M, 8:16], start=True, stop=True))
            kb.op('act', [B2], [dts], lambda e: e.copy(out=dts[0:M, 24:32], in_=B2.f[0:M, 256:264]))
            kb.op('pe', [dts, smask], [B2], lambda e: e.matmul(B2.f[0:16, 272:280], lhsT=seqm, rhs=dts[0:M, 8:16], start=True, stop=True))
            kb.op('act', [B2], [eac], lambda e: e.copy(out=eac[0:16, :], in_=B2.f[0:16, 272:280]))
            seqmT = smask[0:16, 96:160]
            kb.op('pe', [eac, smask], [B2], lambda e: e.matmul(B2.f[0:M, 264:272], lhsT=seqmT, rhs=eac[0:16, :], start=True, stop=True))
            kb.op('act', [B2], [dts], lambda e: e.copy(out=dts[0:M, 32:40], in_=B2.f[0:M, 264:272]))
            kb.op('dve', [dts], [dts], lambda e: e.tensor_tensor(out=dts[0:M, 40:48], in0=dts[0:M, 32:40], in1=dts[0:M, 24:32], op=ALU.subtract))
            kb.op('act', [dts], [dts], lambda e: e.activation(out=dts[0:M, 40:48], in_=dts[0:M, 40:48], func=AF.Exp))
            kb.op('dve', [dts], [dts], lambda e: e.tensor_tensor(out=dts[0:M, 48:56], in0=dts[0:M, 40:48], in1=dts[0:M, 0:8], op=ALU.mult))
            xs3 = xsB[0:M, 0:512].rearrange("p (h d) -> p h d", h=8)
            kb.op('dve', [xsB, dts], [xdte], lambda e: e.tensor_tensor(out=xdte[0:M], in0=xs3, in1=bc(dts[0:M, 48:56].unsqueeze(2), [M, 8, 64]), op=ALU.mult))
            kb.op('dve', [xsB, dts], [xdt], lambda e: e.tensor_tensor(out=xdt[0:M], in0=xs3, in1=bc(dts[0:M, 0:8].unsqueeze(2), [M, 8, 64]), op=ALU.mult))
            kb.op('pool', [xsB, prm], [xskip], lambda e: e.tensor_tensor(out=xskip[0:M], in0=xs3, in1=bc(prm[0:M, 16:24].unsqueeze(2), [M, 8, 64]), op=ALU.mult))
            kb.op('act', [dts], [eac], lambda e: e.activation(out=eac[0:M, :], in_=dts[0:M, 24:32], func=AF.Exp))
            ssd_intra(M, bdf, bdb[0:M, :])
            for h in range(8):
                kb.op('pe', [G4, xdt], [B6], lambda e: e.matmul(B6.f[0:M, h * 64:(h + 1) * 64], lhsT=G4[0:M, h, 0:M], rhs=xdt[0:M, h, :], start=True, stop=True))
            kb.op('dve', [], [CTp], lambda e: e.memset(CTp[:], 0.0))
            ct_t = CTp.t
            dstc = bass.AP(ct_t.tensor, ct_t.offset, [list(ct_t.ap[0]), [1024, 2], [65, 16], [16, 4]])
            g2 = G2[:, 6:8, 0:64]
            srcc = bass.AP(g2.tensor, g2.offset, [list(g2.ap[0]), [128, 2], [1, 16], [16, 4]])
            kb.op('dve', [G2], [CTp], lambda e: e.tensor_copy(out=dstc, in_=srcc))
            b_t = xsB[0:M, 512:768]
            bsrc = bass.AP(b_t.tensor, b_t.offset, [list(b_t.ap[0]), [0, 16], [1, 256]])
            kb.op('dve', [xsB, smask], [Bmk], lambda e: e.tensor_tensor(out=Bmk[0:M].rearrange("p b g n -> p b (g n)"), in0=bsrc, in1=bc(seqm.unsqueeze(2), [M, 16, 256]), op=ALU.mult))
            kb.op('dve', [dts], [yos], lambda e: e.tensor_copy(out=yos[0:M], in_=bc(dts[0:M, 8:16].unsqueeze(2), [M, 8, 64])))
            for jt in range(4):
                lrep = yos[0:M].rearrange("p h d -> p (h d)")[:, jt * 128:(jt + 1) * 128]
                kb.op('pe', [yos, smask], [B7], lambda e: e.matmul(B7.f[:, jt * 16:(jt + 1) * 16], lhsT=lrep, rhs=seqm, start=True, stop=True))
            kb.op('act', [B7], [totT], lambda e: e.activation(out=totT[:].rearrange("p a b -> p (a b)"), in_=B7.f[:, 0:64], func=AF.Exp))
            kb.load('sp', S0s[0], S0s[0][:], state_ssm[0:4].rearrange("b (j p) n -> p b j n", p=128))
            for cch in range(4):
                S0 = S0s[cch % 2]
                if cch + 1 < 4:
                    Sn = S0s[(cch + 1) % 2]
                    kb.load('sp', Sn, Sn[:], state_ssm[(cch + 1) * 4:(cch + 2) * 4].rearrange("b (j p) n -> p b j n", p=128))
                kb.op('dve', [S0], [S0h], lambda e: e.tensor_copy(out=S0h[:], in_=S0[:]))
                for bl in range(4):
                    b = cch * 4 + bl
                    h0T = h0Ts[bl]
                    tbk = B3 if b % 2 == 0 else B2
                    sbk = B5 if b % 2 == 0 else B1
                    for jt in range(4):
                        kb.op('pe', [S0h, ident], [tbk], lambda e: e.transpose(out=tbk.b[:, jt * 128:(jt + 1) * 128], in_=S0h[:, bl, jt, :], identity=ident[:, :]), fin=(jt == 3))
                    kb.op('act', [tbk], [h0T], lambda e: e.copy(out=h0T[:, :], in_=tbk.b[:, 0:512]))
                    for g in range(2):
                        bk = B4 if g == 0 else B7
                        kb.op('pe', [CTp, h0T], [bk], lambda e: e.matmul(bk.f[0:M, 0:256], lhsT=CTp[:, g, b, :], rhs=h0T[:, g * 256:(g + 1) * 256], start=(b == 0), stop=(b == 15)))
                    for jt in range(4):
                        kb.op('pe', [xdte, Bmk], [sbk], lambda e: e.matmul(sbk.f[:, jt * 128:(jt + 1) * 128], lhsT=xdte[0:M].rearrange("p h d -> p (h d)")[:, jt * 128:(jt + 1) * 128], rhs=Bmk[0:M, b, jt // 2, :], start=True, stop=True), fin=(jt == 3))
                    for jt in range(4):
                        kb.op('dve', [S0, totT, sbk], [S0], lambda e: e.scalar_tensor_tensor(out=S0[:, bl, jt, :], in0=S0[:, bl, jt, :], scalar=totT[:, jt, b:b + 1], in1=sbk.f[:, jt * 128:(jt + 1) * 128], op0=ALU.mult, op1=ALU.add))
                kb.store('sp', S0, ss_o[cch * 4:(cch + 1) * 4].rearrange("b (j p) n -> p b j n", p=128), S0[:])
            for g in range(2):
                bk = B4 if g == 0 else B7
                kb.op('dve', [bk, eac], [yos], lambda e: e.tensor_tensor(out=yos[0:M, 4 * g:4 * g + 4, :], in0=bk.f[0:M, 0:256].rearrange("p (h d) -> p h d", h=4), in1=bc(eac[0:M, 4 * g:4 * g + 4].unsqueeze(2), [M, 4, 64]), op=ALU.mult))
            ysf = tmpv[0:M].rearrange("p h d -> p (h d)")
            kb.op('dve', [B6, yos], [F2], lambda e: e.tensor_tensor(out=ysf, in0=B6.f[0:M, :], in1=yos[0:M].rearrange("p h d -> p (h d)"), op=ALU.add))
            kb.op('dve', [F2, xskip], [F2], lambda e: e.tensor_tensor(out=ysf, in0=ysf, in1=xskip[0:M].rearrange("p h d -> p (h d)"), op=ALU.add))
            kb.op('dve', [F2, F2], [F2], lambda e: e.tensor_tensor(out=ysf, in0=ysf, in1=zs, op=ALU.mult))
            kb.op('act', [F2], [G6, st], lambda e: e.activation(out=G6[0:M, 512:1024], in_=ysf, func=AF.Square, accum_out=st[0:M, 4:5]))
            rstd_from_sumsq(4, 5, M, 512)
            kb.op('dve', [F2, st], [G6], lambda e: e.tensor_scalar(out=G6[0:M, 512:1024], in0=ysf, scalar1=st[0:M, 5:6], scalar2=None, op0=ALU.mult))
            dbg("s_ysb", F2, tmpv[0:M])
            dbg("s_yos", yos, yos[0:M])
            dbg("s_mix", G6, G6[0:M])
            out_proj_and_store(M, xsb, Hs[0:M, :], Hs, HnTs[:, :, 0:M], HnTs)

        if ENABLE_SAMPLE:
            kb.barrier()
            sample_phase()
            kb.barrier()
        for sg in range(NB_FULL // SG):
            load_quarter(Wq[0], 0)
            if sg > 0:
                build_dg()
            j0 = sg * SG
            front_F(j0, xf[j0 * 128:(j0 + 1) * 128, :])
            front_P_qkv(j0)
            for _ in front_P_xbc(j0):
                pass
            front_P_z(j0)
            backgen = None
            for b in range(SG):
                j = sg * SG + b
                nxt = (b + 1 < SG)
                hook = (lambda jj=j + 1: front_F(jj, xf[jj * 128:(jj + 1) * 128, :])) if nxt else None

                def early(jj=j + 1):
                    front_P_qkv(jj)
                    yield
                    yield from chain_Q(jj, jj, jj == NB_FULL - 1)
                    yield from front_P_xbc(jj)
                stage_mid(j, b, j, j == NB_FULL - 1, hook, backgen, q_done=(b > 0), early=(early if nxt else None), prest=None)
                backgen = stage_back_gen(j, b)
                next(backgen)
                if nxt:
                    front_P_z(j + 1)
            for _ in backgen:
                pass
            phase_b(sg, ENABLE_SAMPLE and sg == NB_FULL // SG - 1)
        kb.finish('sp')
    return nc


_CACHE = {}


def _rope_table(pos):
    half = 8
    inv = (np.float32(THETA) ** (-np.arange(half, dtype=np.float32) * np.float32(2.0) / np.float32(16))).astype(np.float32)
    ang = pos.astype(np.float32)[:, None] * inv[None, :]
    c = np.cos(ang).astype(np.float32); s = np.sin(ang).astype(np.float32)
    return np.concatenate([c, c, -s, s], axis=1)


def kernel(**inputs):
    f = lambda k: np.ascontiguousarray(np.asarray(inputs[k], dtype=np.float32))
    x_prompt = f('x_prompt'); x_sample = f('x_sample'); meta = f('meta_tokens')
    if 'nc' not in _CACHE:
        _CACHE['nc'] = build_program()
    nc = _CACHE['nc']
    smask = np.zeros((128, 256), np.float32)
    r = np.arange(64)
    tq = r // 16; bq = r % 16
    smask[0:64, 0:64] = ((bq[:, None] == bq[None, :]) & (tq[:, None] <= tq[None, :])).astype(np.float32)
    smask[0:64, 64:80] = (bq[:, None] == np.arange(16)[None, :]).astype(np.float32)
    smask[:, 80:84] = (np.arange(128)[:, None] > np.arange(4)[None, :]).astype(np.float32)
    smask[0:16, 96:160] = (np.arange(16)[:, None] == bq[None, :]).astype(np.float32)
    in_maps = []
    for c in range(8):
        seq, half = c // 2, c % 2
        xw = np.zeros((NB_WARM * 128, D), np.float32)
        rowmask = np.zeros((128, NB_WARM), np.float32)
        rope = np.zeros((128, 18, 32), np.float32)
        rr = np.arange(128)
        if half == 0:
            xw[16 * 128 + 112:17 * 128] = meta
            rowmask[112:, 16] = 1.0
            rope[:, 16, :] = _rope_table(np.maximum(rr - 112, 0))
            base = 16
            xfull = x_prompt[seq, 0:2048]
        else:
            xw[112:128] = meta
            rowmask[112:, 0] = 1.0
            xw[128:] = x_prompt[seq, 0:2048]
            rowmask[:, 1:] = 1.0
            rope[:, 16, :] = _rope_table(16 + 15 * 128 + rr)
            base = 16 + 2048
            xfull = x_prompt[seq, 2048:4096]
        for j in range(16):
            rope[:, j, :] = _rope_table(base + 128 * j + rr)
        rope[0:64, 17, :] = _rope_table(PAST + tq)
        xs_ = x_sample[16 * c:16 * c + 16]
        xsm = np.ascontiguousarray(xs_.transpose(1, 0, 2).reshape(64, D))
        m = {
            "xw": xw, "xf": np.ascontiguousarray(xfull), "xsm": xsm, "rowmask": rowmask, "rope": rope, "smask": smask,
            "w_in": f('w_in')[0], "w_out": f('w_out')[0], "w_up": f('w_up')[0], "w_down": f('w_down')[0],
            "norm1_g": f('norm1_g')[0], "norm2_g": f('norm2_g')[0], "norm_f_g": f('norm_f_g'),
            "attn_out_g": f('attn_out_g')[0], "ssm_out_g": f('ssm_out_g')[0], "attn_sinks": f('attn_sinks')[0],
            "dt_bias": f('dt_bias')[0], "a_log": f('a_log')[0], "d_skip": f('d_skip')[0],
            "conv_w": f('conv_w')[0], "conv_b": f('conv_b')[0],
            "cache_k": np.ascontiguousarray(f('cache_k')[0, 16 * c:16 * c + 16].reshape(16, 128, 128)),
            "cache_v": np.ascontiguousarray(f('cache_v')[0, 16 * c:16 * c + 16].reshape(16, 128, 128)),
            "state_conv": np.ascontiguousarray(f('state_conv')[0, 16 * c:16 * c + 16]),
            "state_ssm": np.ascontiguousarray(f('state_ssm')[0, 16 * c:16 * c + 16].reshape(16, 512, 128)),
        }
        in_maps.append(m)
    res = run_bass_kernel_spmd(nc, in_maps, core_ids=list(range(8)))
    R = res.results
    y_prompt = np.zeros((4, 4096, D), np.float32)
    y_sample = np.zeros((128, 4, D), np.float32)
    k_prompt = np.zeros((1, 4, 128, 2, 64), np.float32); v_prompt = np.zeros_like(k_prompt)
    conv_prompt = np.zeros((1, 4, 3, D), np.float32)
    ssm_prompt = np.zeros((1, 4, 8, 64, 128), np.float32)
    k_sample = np.zeros((1, 128, 128, 2, 64), np.float32); v_sample = np.zeros_like(k_sample)
    conv_sample = np.zeros((1, 128, 3, D), np.float32)
    ssm_sample = np.zeros((1, 128, 8, 64, 128), np.float32)
    for c in range(8):
        seq, half = c // 2, c % 2
        o = R[c]
        y_prompt[seq, half * 2048:(half + 1) * 2048] = o["y_f"]
        y_sample[16 * c:16 * c + 16] = o["y_s"].reshape(4, 16, D).transpose(1, 0, 2)
        if half == 1:
            k_prompt[0, seq] = o["kp_o"].reshape(128, 2, 64)
            v_prompt[0, seq] = o["vp_o"].reshape(128, 2, 64)
            conv_prompt[0, seq] = o["cp_o"]
            ssm_prompt[0, seq] = o["sp_o"].reshape(8, 64, 128)
        k_sample[0, 16 * c:16 * c + 16] = o["ks_o"].reshape(16, 128, 2, 64)
        v_sample[0, 16 * c:16 * c + 16] = o["vs_o"].reshape(16, 128, 2, 64)
        conv_sample[0, 16 * c:16 * c + 16] = o["cs_o"]
        ssm_sample[0, 16 * c:16 * c + 16] = o["ss_o"].reshape(16, 8, 64, 128)
    return (y_prompt, y_sample, k_prompt, v_prompt, conv_prompt, ssm_prompt,
            k_sample, v_sample, conv_sample, ssm_sample)
```
